# Optimizing a Trainium2 kernel written in Bass

```python
import math
import jax, jax.numpy as jnp
from jax import lax
import numpy as np

D_MODEL = 2048
BATCH = 4
SEQ = 4096
DEPTH = 1

HEAD_DIM = 128
MOBA_HEADS = D_MODEL // (2 * HEAD_DIM)
FOX_HEADS = D_MODEL // (2 * HEAD_DIM)
WA = MOBA_HEADS * HEAD_DIM
WB = FOX_HEADS * HEAD_DIM
MOBA_BLOCK = 256
MOBA_TOPK = 3
MOBA_Q_CHUNK = 32
FOX_Q_CHUNK = 128
ROPE_THETA = 10000.0
PLE_DIM = 256
RMS_EPS = 1e-6

_SECTION_WIDTHS = (WA, WA, WA, WA, WB, WB, WB, FOX_HEADS, WB, D_MODEL, D_MODEL)
N_IN = int(sum(_SECTION_WIDTHS))
SPLIT_POINTS = tuple(int(s) for s in np.cumsum(_SECTION_WIDTHS)[:-1])

kernel_name = "hybrid_moba_fox_gated_parallel"


def rmsnorm(x, g):
    xf = x.astype(jnp.float32)
    y = xf * lax.rsqrt(jnp.mean(xf * xf, axis=-1, keepdims=True) + RMS_EPS)
    return (y * g.astype(jnp.float32)).astype(x.dtype)


def rope(x, positions):
    inv = ROPE_THETA ** (-jnp.arange(0, HEAD_DIM, 2, dtype=jnp.float32) / HEAD_DIM)
    ang = positions.astype(jnp.float32)[..., None] * inv
    cos = jnp.cos(ang)[:, :, None, :]
    sin = jnp.sin(ang)[:, :, None, :]
    xf = x.astype(jnp.float32)
    x1, x2 = xf[..., : HEAD_DIM // 2], xf[..., HEAD_DIM // 2:]
    out = jnp.concatenate([x1 * cos - x2 * sin, x2 * cos + x1 * sin], axis=-1)
    return out.astype(x.dtype)


def moba_attention(q, k, v):
    B, H, S, hd = q.shape
    nb = -(-S // MOBA_BLOCK)
    pad = nb * MOBA_BLOCK - S
    kp = jnp.pad(k, ((0, 0), (0, 0), (0, pad), (0, 0)))
    vp = jnp.pad(v, ((0, 0), (0, 0), (0, pad), (0, 0)))
    kb = kp.reshape(B, H, nb, MOBA_BLOCK, hd)
    vb = vp.reshape(B, H, nb, MOBA_BLOCK, hd)
    kbar = jnp.mean(kb.astype(jnp.float32), axis=3)

    t = jnp.arange(S)
    qblk = t // MOBA_BLOCK
    gscore = jnp.einsum('bhsd,bhnd->bhsn', q.astype(jnp.float32), kbar)
    past = jnp.arange(nb)[None, :] < qblk[:, None]
    gscore = jnp.where(past[None, None], gscore, -jnp.inf)
    n_sel = min(MOBA_TOPK, nb)
    _, top_idx = lax.top_k(gscore, n_sel)
    rank_ok = jnp.arange(n_sel)[None, :] < qblk[:, None]
    own = jnp.broadcast_to(qblk[None, None, :, None], (B, H, S, 1))
    idx = jnp.concatenate([top_idx, own.astype(top_idx.dtype)], axis=-1)
    blk_ok = jnp.concatenate([rank_ok, jnp.ones((S, 1), bool)], axis=-1)
    n = n_sel + 1

    nq = S // MOBA_Q_CHUNK
    qc = q.reshape(B, H, nq, MOBA_Q_CHUNK, hd).transpose(2, 0, 1, 3, 4)
    ic = idx.reshape(B, H, nq, MOBA_Q_CHUNK, n).transpose(2, 0, 1, 3, 4)
    tc = t.reshape(nq, MOBA_Q_CHUNK)
    vc = blk_ok.reshape(nq, MOBA_Q_CHUNK, n)
    bi = jnp.arange(B)[:, None, None, None]
    hi = jnp.arange(H)[None, :, None, None]
    scale = 1.0 / math.sqrt(hd)

    def step(args):
        q_c, i_c, t_c, v_c = args
        kg = kb[bi, hi, i_c]
        vg = vb[bi, hi, i_c]
        logits = jnp.einsum('bhqd,bhqnkd->bhqnk', q_c, kg).astype(jnp.float32) * scale
        kpos = i_c[..., None] * MOBA_BLOCK + jnp.arange(MOBA_BLOCK)
        mask = (kpos <= t_c[None, None, :, None, None]) & v_c[None, None, :, :, None]
        logits = jnp.where(mask, logits, -jnp.inf)
        w = jax.nn.softmax(logits.reshape(B, H, MOBA_Q_CHUNK, n * MOBA_BLOCK), axis=-1)
        w = w.reshape(B, H, MOBA_Q_CHUNK, n, MOBA_BLOCK).astype(v.dtype)
        return jnp.einsum('bhqnk,bhqnkd->bhqd', w, vg)

    out = lax.map(step, (qc, ic, tc, vc))
    return out.transpose(1, 2, 0, 3, 4).reshape(B, H, S, hd)


def fox_attention(q, k, v, logf):
    B, H, S, hd = q.shape
    c = lax.cumsum(logf, axis=2)
    nq = S // FOX_Q_CHUNK
    qc = q.reshape(B, H, nq, FOX_Q_CHUNK, hd).transpose(2, 0, 1, 3, 4)
    cq = c.reshape(B, H, nq, FOX_Q_CHUNK).transpose(2, 0, 1, 3)
    tc = jnp.arange(S).reshape(nq, FOX_Q_CHUNK)
    spos = jnp.arange(S)
    scale = 1.0 / math.sqrt(hd)

    def step(args):
        q_c, c_c, t_c = args
        logits = jnp.einsum('bhqd,bhsd->bhqs', q_c, k).astype(jnp.float32) * scale
        logits = logits + (c_c[..., None] - c[:, :, None, :])
        mask = spos[None, :] <= t_c[:, None]
        logits = jnp.where(mask[None, None], logits, -jnp.inf)
        w = jax.nn.softmax(logits, axis=-1).astype(v.dtype)
        return jnp.einsum('bhqs,bhsd->bhqd', w, v)

    out = lax.map(step, (qc, cq, tc))
    return out.transpose(1, 2, 0, 3, 4).reshape(B, H, S, hd)


def setup_inputs(seed: int = 0) -> dict:
    key = jax.random.key(seed)
    ks = jax.random.split(key, 14)
    f32 = jnp.float32
    x = jax.random.normal(ks[0], (BATCH, SEQ, D_MODEL), f32)
    p = jax.random.normal(ks[1], (DEPTH, BATCH, SEQ, PLE_DIM), f32)
    positions = jnp.broadcast_to(jnp.arange(SEQ, dtype=jnp.int32)[None, :], (BATCH, SEQ))
    g_norm = 1.0 + 0.02 * jax.random.normal(ks[2], (DEPTH, D_MODEL), f32)
    w_in = jax.random.normal(ks[3], (DEPTH, D_MODEL, N_IN), f32) * D_MODEL ** -0.5
    b_f = 1.0 + 0.1 * jax.random.normal(ks[4], (DEPTH, FOX_HEADS), f32)
    w_branch_a = jax.random.normal(ks[5], (DEPTH, WA, D_MODEL), f32) * WA ** -0.5
    w_branch_b = jax.random.normal(ks[6], (DEPTH, WB, D_MODEL), f32) * WB ** -0.5
    w_out = jax.random.normal(ks[7], (DEPTH, D_MODEL, D_MODEL), f32) * D_MODEL ** -0.5
    g_ple = 1.0 + 0.02 * jax.random.normal(ks[8], (DEPTH, D_MODEL), f32)
    w_ple_gate = jax.random.normal(ks[9], (DEPTH, D_MODEL, D_MODEL), f32) * D_MODEL ** -0.5
    w_ple_up = jax.random.normal(ks[10], (DEPTH, PLE_DIM, D_MODEL), f32) * PLE_DIM ** -0.5
    g_final = 1.0 + 0.02 * jax.random.normal(ks[11], (D_MODEL,), f32)
    return {"x": x, "p": p, "positions": positions, "g_norm": g_norm, "w_in": w_in,
            "b_f": b_f, "w_branch_a": w_branch_a, "w_branch_b": w_branch_b, "w_out": w_out,
            "g_ple": g_ple, "w_ple_gate": w_ple_gate, "w_ple_up": w_ple_up, "g_final": g_final}


def reference(x, p, positions, g_norm, w_in, b_f, w_branch_a, w_branch_b, w_out,
              g_ple, w_ple_gate, w_ple_up, g_final):
    B, S, _ = x.shape
    for i in range(DEPTH):
        h = rmsnorm(x, g_norm[i])
        proj = h @ w_in[i]
        qa, ka, va, za, qb, kb, vb, fb, zb, ga, gb = jnp.split(proj, SPLIT_POINTS, axis=-1)

        qa = rope(qa.reshape(B, S, MOBA_HEADS, HEAD_DIM), positions).transpose(0, 2, 1, 3)
        ka = rope(ka.reshape(B, S, MOBA_HEADS, HEAD_DIM), positions).transpose(0, 2, 1, 3)
        va = va.reshape(B, S, MOBA_HEADS, HEAD_DIM).transpose(0, 2, 1, 3)
        oa = moba_attention(qa, ka, va).transpose(0, 2, 1, 3).reshape(B, S, WA)
        ya = (oa * jax.nn.silu(za)) @ w_branch_a[i]

        qb = qb.reshape(B, S, FOX_HEADS, HEAD_DIM).transpose(0, 2, 1, 3)
        kb = kb.reshape(B, S, FOX_HEADS, HEAD_DIM).transpose(0, 2, 1, 3)
        vb = vb.reshape(B, S, FOX_HEADS, HEAD_DIM).transpose(0, 2, 1, 3)
        logf = jax.nn.log_sigmoid((fb + b_f[i]).astype(jnp.float32)).transpose(0, 2, 1)
        ob = fox_attention(qb, kb, vb, logf).transpose(0, 2, 1, 3).reshape(B, S, WB)
        yb = (ob * jax.nn.silu(zb)) @ w_branch_b[i]

        mixed = jax.nn.sigmoid(ga) * ya + jax.nn.sigmoid(gb) * yb
        x = x + mixed @ w_out[i]

        pg = jax.nn.sigmoid(rmsnorm(x, g_ple[i]) @ w_ple_gate[i])
        x = x + (p[i] @ w_ple_up[i]) * pg
    return rmsnorm(x, g_final)
```

```python
import math
from contextlib import ExitStack

import numpy as np
import concourse.bass as bass
import concourse.mybir as mybir
from concourse.bass_utils import run_bass_kernel_spmd

F32 = mybir.dt.float32
BF16 = mybir.dt.bfloat16
I32 = mybir.dt.int32
AF = mybir.ActivationFunctionType
ALU = mybir.AluOpType
AX = mybir.AxisListType

D = 2048
NCH = 16
T = 2048
CT = 2048
HD = 128
NH = 8
NEG = -30000.0
MAGIC = 12582912.0
TWO_PI = 2.0 * math.pi
C1 = 6.28125
C2 = TWO_PI - C1
PI_SAFE = 3.1415925


class Buf:
    __slots__ = ("name", "w", "r")

    def __init__(self, name):
        self.name = name
        self.w = None
        self.r = []


class Op:
    __slots__ = ("eng", "fn", "deps", "signal", "tok_sem", "tok_val", "is_dma", "final", "seq")

    def __init__(self, eng, fn, is_dma=False):
        self.eng = eng
        self.fn = fn
        self.deps = []
        self.signal = False
        self.tok_sem = None
        self.tok_val = None
        self.is_dma = is_dma
        self.final = False


class Sched:
    ENGS = ("pe", "act", "dve", "pool", "sp")

    def __init__(self, nc):
        self.nc = nc
        self.ops = {e: [] for e in self.ENGS}
        self.streams = {}
        self.finals = []
        self.bar_ops = []
        self.bar_taken = set(self.ENGS)
        self.dma_since_bar = []
        self.seq = 0

    def _add_deps(self, op, reads, writes):
        self.seq += 1
        op.seq = self.seq
        cands = []
        for b in reads:
            if b.w is not None:
                cands.append((b.w, True))
        for b in writes:
            if b.w is not None:
                cands.append((b.w, False))
            for r in b.r:
                cands.append((r, False))
        if op.eng not in self.bar_taken:
            self.bar_taken.add(op.eng)
            for d in self.bar_ops:
                cands.append((d, True))
        best = {}
        for d, raw in cands:
            if d is op:
                continue
            if (not d.is_dma) and (not op.is_dma) and d.eng == op.eng:
                if op.eng == "pe":
                    continue
            key = d.tok_sem if d.is_dma else ("eng", d.eng)
            cur = best.get(key)
            if cur is None or d.seq > cur.seq:
                best[key] = d
        for d in best.values():
            op.deps.append(d)
            d.signal = True
        for b in reads:
            b.r.append(op)
        for b in writes:
            b.w = op
            b.r = []

    def op(self, eng, fn, reads=(), writes=()):
        o = Op(eng, fn)
        self._add_deps(o, list(reads), list(writes))
        self.ops[eng].append(o)
        return o

    def dma(self, queue, out, in_, reads=(), writes=(), stream=None, final=False):
        o = Op(queue, (out, in_), is_dma=True)
        o.signal = True
        n = self.streams.get(stream, 0) + 1
        self.streams[stream] = n
        o.tok_sem = stream
        o.tok_val = 16 * n
        o.final = final
        self._add_deps(o, list(reads), list(writes))
        self.ops[queue].append(o)
        self.dma_since_bar.append(o)
        if final:
            self.finals.append(o)
        return o

    def barrier(self):
        last = {}
        for o in self.dma_since_bar:
            last[o.tok_sem] = o
        bar = list(last.values())
        for e in self.ENGS:
            for o in reversed(self.ops[e]):
                if not o.is_dma:
                    bar.append(o)
                    break
        self.bar_ops = bar
        self.bar_taken = set()
        self.dma_since_bar = []

    def emit(self):
        nc = self.nc
        for e in self.ENGS:
            cnt = 0
            for o in self.ops[e]:
                if o.is_dma:
                    continue
                if o.signal:
                    cnt += 1
                    o.tok_sem = "eng_" + e
                    o.tok_val = cnt
        with ExitStack() as es:
            sems = {}
            for e in self.ENGS:
                sems["eng_" + e] = es.enter_context(nc.semaphore("s_" + e))
            for s in self.streams:
                sems[s] = es.enter_context(nc.semaphore("d_" + s))
            block = es.enter_context(nc.Block())
            handles = {"pe": block.tensor, "act": block.scalar, "dve": block.vector,
                       "pool": block.gpsimd, "sp": block.sync}
            finals = self.finals

            def make(e):
                def body(eng):
                    known = {}
                    for o in self.ops[e]:
                        for d in o.deps:
                            if known.get(d.tok_sem, 0) < d.tok_val:
                                eng.wait_ge(sems[d.tok_sem], d.tok_val)
                                known[d.tok_sem] = d.tok_val
                        if o.is_dma:
                            out, in_ = o.fn
                            eng.dma_start(out=out, in_=in_).then_inc(sems[o.tok_sem], 16)
                        else:
                            ins = o.fn(eng)
                            if o.signal:
                                ins.then_inc(sems[o.tok_sem], 1)
                    if e == "sp":
                        for o in finals:
                            if known.get(o.tok_sem, 0) < o.tok_val:
                                eng.wait_ge(sems[o.tok_sem], o.tok_val)
                                known[o.tok_sem] = o.tok_val
                return body
            for e in self.ENGS:
                handles[e](make(e))


class _Stop(Exception):
    pass


def build_program(dbg=False, stop=None, nheads=NH, ctile=0, wseq=None, rec=None):
    nc = bass.Bass("TRN2", target_bir_lowering=False)
    dumps = []

    def ck(k, items=()):
        if stop == k:
            for name, ap, bufs in items:
                d = nc.dram_tensor("dbg_" + name, list(ap.shape), ap.dtype, kind="ExternalOutput").ap()
                S.dma("sp", d, ap, reads=bufs, writes=[], stream="dbg_" + name, final=True)
            raise _Stop()

    def din(name, shape, dt=F32):
        return nc.dram_tensor(name, list(shape), dt, kind="ExternalInput").ap()

    x_all = din("x_all", [CT + T, D])
    pos_all = din("pos_all", [1, CT + T], I32)
    p_own = din("p_own", [T, 256])
    w_main = din("w_main", [96, 128, NCH, 128])
    w_f = din("w_f", [128, NCH, 8])
    w_ba = din("w_ba", [16, 128, 8, 128])
    w_bb = din("w_bb", [16, 128, 8, 128])
    w_o = din("w_o", [16, 128, NCH, 128])
    w_pg = din("w_pg", [16, 128, NCH, 128])
    w_up = din("w_up", [16, 128, 2, 128])
    g_norm = din("g_norm", [128, NCH])
    g_ple = din("g_ple", [128, NCH])
    g_fin = din("g_fin", [128, D])
    b_f = din("b_f", [8, 1])
    c_ident = din("c_ident", [128, 128])
    c_rotT = din("c_rotT", [128, 128])
    c_inv = din("c_inv", [128, 1])
    c_cmask = din("c_cmask", [128, 4, 512])
    c_E = din("c_E", [16, 16, 128])
    c_oh = din("c_oh", [8, 8, 128])
    c_ctxneg = din("c_ctxneg", [128, 256])
    c_ctx30 = din("c_ctx30", [128, 256])
    c_past = din("c_past", [128, 256])
    c_ctxcol = din("c_ctxcol", [128, 1])
    out_d = nc.dram_tensor("out", [T, D], F32, kind="ExternalOutput").ap()

    skind = "ExternalOutput" if dbg else "Internal"
    KT_s = nc.dram_tensor("KT_s", [2, NH, 128, CT], BF16, kind=skind).ap()
    V_s = nc.dram_tensor("V_s", [2, NH, 128, 16, 129], BF16, kind=skind).ap()
    GT_s = nc.dram_tensor("GT_s", [128, 16, T], BF16, kind=skind).ap()
    SG_s = nc.dram_tensor("SG_s", [32, 128, T], BF16, kind=skind).ap()
    WS = {"ba": nc.dram_tensor("WS_ba", [16, 128, 1024], BF16, kind=skind).ap(),
          "bb": nc.dram_tensor("WS_bb", [16, 128, 1024], BF16, kind=skind).ap(),
          "o": nc.dram_tensor("WS_o", [16, 128, 2048], BF16, kind=skind).ap(),
          "pg": nc.dram_tensor("WS_pg", [16, 128, 2048], BF16, kind=skind).ap(),
          "up": nc.dram_tensor("WS_up", [16, 128, 256], BF16, kind=skind).ap()}

    S = Sched(nc)
    es = ExitStack()
    KTs_b = [[Buf("KTs") for _ in range(NH)] for _ in range(2)]
    Vs_b = [[Buf("Vs") for _ in range(NH)] for _ in range(2)]
    GTs_b = [[Buf("GTs") for _ in range(4)] for _ in range(16)]
    SGs_b = [Buf("SGs") for _ in range(32)]
    WSb = {k: [Buf("WS" + k) for _ in range(16)] for k in ("ba", "bb", "o", "pg", "up")}

    def sb(name, shape, dt):
        return es.enter_context(nc.sbuf_tensor(name, list(shape), dt))

    hT = sb("hT", [128, NCH, T], BF16)
    hT_b = [Buf("hT%d" % i) for i in range(16)]
    stage = [sb("stage%d" % i, [128, D], F32) for i in range(3)]
    stage_b = [Buf("stage%d" % i) for i in range(3)]
    wbf = [sb("wbf%d" % i, [128, NCH, 128], BF16) for i in range(3)]
    wbf_b = [Buf("wbf%d" % i) for i in range(3)]
    ident32 = sb("ident32", [128, 128], F32); ident32_b = Buf("ident32")
    identb = sb("identb", [128, 128], BF16); identb_b = Buf("identb")
    rotT = sb("rotT", [128, 128], F32); rotT_b = Buf("rotT")
    ones32 = sb("ones32", [128, 128], F32); ones32_b = Buf("ones32")
    cmaskb = sb("cmaskb", [128, 4, 512], BF16); cmaskb_b = Buf("cmaskb")
    Eb = sb("Eb", [128, 16, 128], BF16); Eb_b = Buf("Eb")
    ohb = sb("ohb", [128, 8, 128], BF16); ohb_b = Buf("ohb")
    gn = sb("gn", [128, NCH], F32); gn_b = Buf("gn")
    gp = sb("gp", [128, NCH], F32); gp_b = Buf("gp")
    inv = sb("inv", [128, 1], F32); inv_b = Buf("inv")
    bfn = sb("bfn", [8, 1], F32); bfn_b = Buf("bfn")
    ctxneg = sb("ctxneg", [128, 256], F32); ctxneg_b = Buf("ctxneg")
    ctx30 = sb("ctx30", [128, 256], F32); ctx30_b = Buf("ctx30")
    pastm = sb("pastm", [128, 256], F32); pastm_b = Buf("pastm")
    ctxcol = sb("ctxcol", [128, 1], F32); ctxcol_b = Buf("ctxcol")
    wfb = sb("wfb", [128, NCH, 8], BF16); wfb_b = Buf("wfb")
    ones1 = sb("ones1", [128, 1], F32); ones1_b = Buf("ones1")

    AR_W = 22 * 1024 + 320
    arena = sb("arena", [128, AR_W], F32)
    ar_off = [0]

    def carve(nwords):
        a = ar_off[0]
        assert a + nwords <= AR_W, (a, nwords)
        ar_off[0] = a + nwords
        return arena[:, a:a + nwords]

    ps = [es.enter_context(nc.psum_tensor("ps%d" % i, [128, 512], F32)) for i in range(8)]
    ps_b = [Buf("ps%d" % i) for i in range(8)]

    dcnt = [0]

    def dstream(prefix):
        return prefix

    def load_const(dst, src, b, name):
        S.dma("sp", dst, src, writes=[b], stream="c_" + name)

    load_const(ident32[:], c_ident, ident32_b, "ident")
    load_const(rotT[:], c_rotT, rotT_b, "rot")
    load_const(gn[:], g_norm, gn_b, "gn")
    load_const(gp[:], g_ple, gp_b, "gp")
    load_const(inv[:], c_inv, inv_b, "inv")
    load_const(ctxneg[:], c_ctxneg, ctxneg_b, "ctxneg")
    load_const(ctx30[:], c_ctx30, ctx30_b, "ctx30")
    load_const(pastm[:], c_past, pastm_b, "past")
    load_const(ctxcol[:], c_ctxcol, ctxcol_b, "ctxcol")
    load_const(bfn[:], b_f, bfn_b, "bf")
    S.op("pool", lambda e: e.tensor_scalar(out=bfn[:], in0=bfn[:], scalar1=-1.0, scalar2=0.0, op0=ALU.mult, op1=ALU.add),
         reads=[bfn_b], writes=[bfn_b])
    S.op("pool", lambda e: e.memset(ones32[:], 1.0), writes=[ones32_b])
    S.op("pool", lambda e: e.memset(ones1[:], 1.0), writes=[ones1_b])
    S.op("pool", lambda e: e.tensor_copy(out=identb[:], in_=ident32[:]), reads=[ident32_b], writes=[identb_b])
    S.dma("sp", stage[0][:], c_cmask.rearrange("p a b -> p (a b)"), writes=[stage_b[0]], stream="st0")
    S.op("pool", lambda e: e.tensor_copy(out=cmaskb[:].rearrange("p a b -> p (a b)"), in_=stage[0][:]),
         reads=[stage_b[0]], writes=[cmaskb_b])
    S.dma("sp", stage[1][0:16, :], c_E.rearrange("p a b -> p (a b)"), writes=[stage_b[1]], stream="st1")
    S.op("pool", lambda e: e.memset(Eb[:].rearrange("p a b -> p (a b)"), 0.0), writes=[Eb_b])
    S.op("pool", lambda e: e.tensor_copy(out=Eb[0:16].rearrange("p a b -> p (a b)"), in_=stage[1][0:16, :]),
         reads=[stage_b[1]], writes=[Eb_b])
    S.dma("sp", stage[2][0:8, 0:1024], c_oh.rearrange("p a b -> p (a b)"), writes=[stage_b[2]], stream="st2")
    S.op("pool", lambda e: e.memset(ohb[:].rearrange("p a b -> p (a b)"), 0.0), writes=[ohb_b])
    S.op("pool", lambda e: e.tensor_copy(out=ohb[0:8].rearrange("p a b -> p (a b)"), in_=stage[2][0:8, 0:1024]),
         reads=[stage_b[2]], writes=[ohb_b])
    S.dma("sp", stage[0][:, 0:128], w_f.rearrange("p a b -> p (a b)"), writes=[stage_b[0]], stream="st0")
    S.op("pool", lambda e: e.tensor_tensor(out=wfb[:], in0=stage[0][:, 0:128].rearrange("p (a b) -> p a b", b=8),
                                           in1=gn[:].unsqueeze(2).broadcast_to([128, NCH, 8]), op=ALU.mult),
         reads=[stage_b[0], gn_b], writes=[wfb_b])

    def bfv(ap):
        return ap.bitcast(BF16)

    KT = bfv(carve(2048)); KTctx_b = Buf("KTctx"); KTown_b = [Buf("KTown%d" % i) for i in range(4)]
    Vf = bfv(carve(2064))[:, 0:32 * 129].rearrange("p (t d) -> p t d", d=129)
    Vctx_b = Buf("Vctx"); Vown_b = [Buf("Vown%d" % i) for i in range(4)]
    Vones_b = Buf("Vones")
    QT = [bfv(carve(1024)) for _ in range(2)]
    QT_b = [[Buf("QT%d_%d" % (i, j)) for j in range(4)] for i in range(2)]
    zsT = [bfv(carve(1024)) for _ in range(2)]
    zsT_b = [[Buf("zs%d_%d" % (i, j)) for j in range(4)] for i in range(2)]
    cosT = carve(2048); cosT_b = Buf("cosT")
    sinT = carve(2048); sinT_b = Buf("sinT")
    k32 = [carve(512) for _ in range(2)]; k32_b = [Buf("k32_%d" % i) for i in range(2)]
    rt1 = [carve(512) for _ in range(2)]; rt1_b = [Buf("rt1_%d" % i) for i in range(2)]
    PT = [bfv(carve(256)) for _ in range(3)]; PT_b = [Buf("PT%d" % i) for i in range(3)]
    chat_full = bfv(carve(2048)); chat = chat_full[0:8, :]; chat_b = [Buf("chat%d" % i) for i in range(8)]
    negc = carve(256).rearrange("p (t h) -> p t h", h=8); negc_b = [Buf("negc%d" % i) for i in range(8)]
    selbT_full = bfv(carve(1024)); selbT = selbT_full[0:16, :]; selbT_b = [Buf("selbT%d" % i) for i in range(4)]
    gsb = carve(256); gsb_b = Buf("gsb")
    selb = carve(256); selb_b = Buf("selb")
    m8 = carve(128); m8_b = Buf("m8")
    kbar32 = carve(16); kbar32_b = Buf("kbar32")
    kbarb = bfv(carve(8)); kbarb_b = Buf("kbarb")
    On = [bfv(carve(64)) for _ in range(2)]; On_b = [Buf("On%d" % i) for i in range(2)]
    rden = [carve(1) for _ in range(2)]; rden_b = [Buf("rden%d" % i) for i in range(2)]
    GTt = [bfv(carve(256)) for _ in range(2)]; GTt_b = [Buf("GTt%d" % i) for i in range(2)]
    junk = bfv(cosT[:, 0:1024]); junk_b = cosT_b
    ssq = [carve(1) for _ in range(2)]; ssq_b = [Buf("ssq%d" % i) for i in range(2)]
    fe = [carve(512)[0:8, :] for _ in range(2)]; fe_b = [Buf("fe%d" % i) for i in range(2)]
    cst = [carve(512)[0:8, :] for _ in range(2)]; cst_b = [Buf("cst%d" % i) for i in range(2)]
    cslast = carve(1)[0:8, :]; cslast_b = Buf("cslast")
    ones8 = carve(512)[0:8, :]; ones8_b = Buf("ones8")
    abend = ar_off[0]

    S.op("pool", lambda e: e.memset(Vf[:, :, 128:129], 1.0), writes=[Vones_b])
    S.op("pool", lambda e: e.memset(chat_full, 0.0), writes=chat_b)
    S.op("pool", lambda e: e.memset(selbT_full, 0.0), writes=selbT_b)
    S.op("pool", lambda e: e.memset(ones8, 1.0), writes=[ones8_b])
    S.op("pool", lambda e: e.memset(cslast, 0.0), writes=[cslast_b])

    wctr = [0]

    wptr = [0]
    issued = {}

    def _issue(key, slot):
        name, blk, nchunk, gk, pz = key
        gfold, gfold_b = WG[gk]
        wb = wbf[slot % 3]; wbb = wbf_b[slot % 3]
        n = nchunk * 128
        if pz is not None and pz >= 1:
            S.dma("sp", wb[:, 0:nchunk, :].rearrange("p a b -> p (a b)"), WS[name][blk], reads=[WSb[name][blk]], writes=[wbb],
                  stream="wl%d" % (slot % 3))
            return
        src_blk = WSRC[name][blk]
        i = wctr[0]
        wctr[0] += 1
        st = stage[i % 3]; stb = stage_b[i % 3]
        S.dma("sp", st[:, 0:n], src_blk.rearrange("p a b -> p (a b)"), writes=[stb], stream="st%d" % (i % 3))
        ceng = "pool" if slot % 2 == 0 else "dve"
        if gfold is not None:
            S.op(ceng, lambda e: e.tensor_tensor(out=wb[:, 0:nchunk, :], in0=st[:, 0:n].rearrange("p (a b) -> p a b", b=128),
                                                 in1=gfold[:, 0:nchunk].unsqueeze(2).broadcast_to([128, nchunk, 128]), op=ALU.mult),
                 reads=[stb, gfold_b], writes=[wbb])
        else:
            S.op(ceng, lambda e: e.tensor_copy(out=wb[:, 0:nchunk, :], in_=st[:, 0:n].rearrange("p (a b) -> p a b", b=128)),
                 reads=[stb], writes=[wbb])
        if pz == 0:
            S.dma("pool", WS[name][blk], wb[:, 0:nchunk, :].rearrange("p a b -> p (a b)"), reads=[wbb], writes=[WSb[name][blk]],
                  stream="wst%d" % (slot % 3))

    WSRC = {"main": w_main, "ba": w_ba, "bb": w_bb, "o": w_o, "pg": w_pg, "up": w_up}
    WG = {"n": (gn, gn_b), "p": (gp, gp_b), "-": (None, None)}

    def load_weight(name, blk, nchunk, gk, pz=None):
        k = wptr[0]
        wptr[0] += 1
        key = (name, blk, nchunk, gk, pz)
        if rec is not None:
            rec.append(key)
        if k not in issued:
            issued[k] = True
            _issue(key, k)
        for kk in (k + 1, k + 2):
            if wseq is not None and kk < len(wseq) and kk not in issued:
                issued[kk] = True
                _issue(wseq[kk], kk)
        return wbf[k % 3], wbf_b[k % 3]

    psrot = [0]

    def next_ps(lo=0, hi=2):
        k = lo + psrot[0] % (hi - lo)
        psrot[0] += 1
        return k

    xctr = [0]

    def phase0(row0):
        for tl in range(16):
            i = wctr[0]; wctr[0] += 1
            st = stage[i % 3]; stb = stage_b[i % 3]
            S.dma("sp", st[:], x_all[row0 + tl * 128: row0 + (tl + 1) * 128, :], writes=[stb], stream="st%d" % (i % 3))
            sq = ssq[tl % 2]; sqb = ssq_b[tl % 2]
            S.op("act", lambda e, st=st, sq=sq: e.activation(out=junk, in_=st[:], func=AF.Square, accum_out=sq),
                 reads=[stb], writes=[junk_b, sqb])
            S.op("act", lambda e, sq=sq: e.activation(out=sq, in_=sq, func=AF.Sqrt, scale=1.0 / D, bias=1e-6),
                 reads=[sqb], writes=[sqb])
            S.op("dve", lambda e, sq=sq: e.reciprocal(out=sq, in_=sq), reads=[sqb], writes=[sqb])
            S.op("dve", lambda e, st=st, sq=sq: e.tensor_scalar(out=st[:], in0=st[:], scalar1=sq, scalar2=None, op0=ALU.mult),
                 reads=[stb, sqb], writes=[stb])
            base = 4 * (tl % 2)
            for b4 in range(4):
                pk = base + b4
                for c4 in range(4):
                    c = b4 * 4 + c4
                    S.op("pe", lambda e, pk=pk, c4=c4, c=c, st=st: e.transpose(out=ps[pk][:, c4 * 128:(c4 + 1) * 128],
                                                                              in_=st[:, c * 128:(c + 1) * 128], identity=ident32[:]),
                         reads=[stb, ident32_b], writes=[ps_b[pk]])
                eng = "act" if b4 % 2 == 0 else "dve"
                if eng == "act":
                    S.op("act", lambda e, pk=pk, b4=b4, tl=tl: e.activation(
                        out=hT[:, b4 * 4:(b4 + 1) * 4, tl * 128:(tl + 1) * 128],
                        in_=ps[pk][:].rearrange("p (a b) -> p a b", b=128), func=AF.Copy),
                        reads=[ps_b[pk]], writes=[hT_b[tl]])
                else:
                    S.op("dve", lambda e, pk=pk, b4=b4, tl=tl: e.tensor_copy(
                        out=hT[:, b4 * 4:(b4 + 1) * 4, tl * 128:(tl + 1) * 128],
                        in_=ps[pk][:].rearrange("p (a b) -> p a b", b=128)),
                        reads=[ps_b[pk]], writes=[hT_b[tl]])

    def rope_tables(p0):
        i0 = wctr[0]; wctr[0] += 3
        sA, sAb = stage[i0 % 3], stage_b[i0 % 3]
        sB, sBb = stage[(i0 + 1) % 3], stage_b[(i0 + 1) % 3]
        sC, sCb = stage[(i0 + 2) % 3], stage_b[(i0 + 2) % 3]
        S.dma("sp", sA[:].bitcast(I32), pos_all[0:1, p0:p0 + 2048].broadcast_to([128, 2048]), writes=[sAb], stream="st%d" % (i0 % 3))
        S.op("dve", lambda e: e.tensor_copy(out=sB[:], in_=sA[:].bitcast(I32)), reads=[sAb], writes=[sBb])
        S.op("dve", lambda e: e.tensor_scalar(out=sB[:], in0=sB[:], scalar1=inv[:, 0:1], scalar2=None, op0=ALU.mult),
             reads=[sBb, inv_b], writes=[sBb])
        S.op("dve", lambda e: e.tensor_scalar(out=sA[:], in0=sB[:], scalar1=1.0 / TWO_PI, scalar2=MAGIC, op0=ALU.mult, op1=ALU.add),
             reads=[sBb], writes=[sAb])
        S.op("dve", lambda e: e.tensor_scalar(out=sA[:], in0=sA[:], scalar1=-MAGIC, scalar2=None, op0=ALU.add),
             reads=[sAb], writes=[sAb])
        S.op("dve", lambda e: e.scalar_tensor_tensor(out=sB[:], in0=sA[:], scalar=-C1, in1=sB[:], op0=ALU.mult, op1=ALU.add),
             reads=[sAb, sBb], writes=[sBb])
        S.op("dve", lambda e: e.scalar_tensor_tensor(out=sB[:], in0=sA[:], scalar=-C2, in1=sB[:], op0=ALU.mult, op1=ALU.add),
             reads=[sAb, sBb], writes=[sBb])
        S.op("dve", lambda e: e.tensor_scalar(out=sB[:], in0=sB[:], scalar1=PI_SAFE, scalar2=-PI_SAFE, op0=ALU.min, op1=ALU.max),
             reads=[sBb], writes=[sBb])
        S.op("act", lambda e: e.activation(out=sinT, in_=sB[:], func=AF.Sin), reads=[sBb], writes=[sinT_b])
        S.op("dve", lambda e: e.tensor_scalar(out=sC[:], in0=sB[:], scalar1=math.pi / 2, scalar2=-TWO_PI, op0=ALU.is_gt, op1=ALU.mult),
             reads=[sBb], writes=[sCb])
        S.op("dve", lambda e: e.scalar_tensor_tensor(out=sC[:], in0=sB[:], scalar=math.pi / 2, in1=sC[:], op0=ALU.add, op1=ALU.add),
             reads=[sBb, sCb], writes=[sCb])
        S.op("dve", lambda e: e.tensor_scalar(out=sC[:], in0=sC[:], scalar1=PI_SAFE, scalar2=-PI_SAFE, op0=ALU.min, op1=ALU.max),
             reads=[sCb], writes=[sCb])
        S.op("act", lambda e: e.activation(out=cosT, in_=sC[:], func=AF.Sin), reads=[sCb], writes=[cosT_b])

    def proj_fm(wb, wbb, nchunk, rhs_fn, rhs_bufs_fn, epilogue, ntile=4):
        for tt in range(ntile):
            pk = next_ps(0, 2)
            for c in range(nchunk):
                S.op("pe", lambda e, pk=pk, c=c, tt=tt: e.matmul(ps[pk][:], lhsT=wb[:, c, :], rhs=rhs_fn(c, tt),
                                                                start=(c == 0), stop=(c == nchunk - 1)),
                     reads=[wbb] + rhs_bufs_fn(tt), writes=[ps_b[pk]])
            epilogue(tt, pk)

    def hT_rhs(c, tt):
        return hT[:, c, tt * 512:(tt + 1) * 512]

    def hT_bufs(tt):
        return hT_b[tt * 4:(tt + 1) * 4]

    ropectr = [0]

    def rope_epilogue(dst_fn, dst_buf_fn, scale):
        def ep(tt, pk):
            i = ropectr[0] % 2; ropectr[0] += 1
            S.op("act", lambda e: e.activation(out=k32[i], in_=ps[pk][:], func=AF.Copy, scale=scale),
                 reads=[ps_b[pk]], writes=[k32_b[i]])
            pr = next_ps(0, 2)
            S.op("pe", lambda e: e.matmul(ps[pr][:], lhsT=rotT[:], rhs=k32[i], start=True, stop=True),
                 reads=[rotT_b, k32_b[i]], writes=[ps_b[pr]])
            S.op("pool", lambda e: e.tensor_tensor(out=rt1[i], in0=k32[i], in1=cosT[:, tt * 512:(tt + 1) * 512], op=ALU.mult),
                 reads=[k32_b[i], cosT_b], writes=[rt1_b[i]])
            S.op("dve", lambda e: e.tensor_tensor(out=k32[i], in0=ps[pr][:], in1=sinT[:, tt * 512:(tt + 1) * 512], op=ALU.mult),
                 reads=[ps_b[pr], sinT_b], writes=[k32_b[i]])
            S.op("dve", lambda e: e.tensor_tensor(out=dst_fn(tt), in0=rt1[i], in1=k32[i], op=ALU.add),
                 reads=[rt1_b[i], k32_b[i]], writes=[dst_buf_fn(tt)])
        return ep

    def copy_epilogue(dst_fn, dst_buf_fn, scale=1.0, func=None):
        def ep(tt, pk):
            S.op("act", lambda e: e.activation(out=dst_fn(tt), in_=ps[pk][:], func=(func or AF.Copy), scale=scale),
                 reads=[ps_b[pk]], writes=[dst_buf_fn(tt)])
        return ep

    def proj_v(wb, wbb, tile0, vbuf_fn):
        for g4 in range(4):
            pk = next_ps(0, 2)
            for t4 in range(4):
                tk = g4 * 4 + t4
                for c in range(NCH):
                    S.op("pe", lambda e, pk=pk, t4=t4, tk=tk, c=c: e.matmul(
                        ps[pk][:, t4 * 128:(t4 + 1) * 128], lhsT=hT[:, c, tk * 128:(tk + 1) * 128], rhs=wb[:, c, :],
                        start=(c == 0 and t4 == 0), stop=(c == NCH - 1), skip_group_check=True),
                        reads=[wbb, hT_b[tk]], writes=[ps_b[pk]])
            S.op("act", lambda e, pk=pk, g4=g4: e.activation(
                out=Vf[:, tile0 + g4 * 4: tile0 + (g4 + 1) * 4, 0:128],
                in_=ps[pk][:].rearrange("p (a b) -> p a b", b=128), func=AF.Copy),
                reads=[ps_b[pk]], writes=[vbuf_fn(g4)])

    def proj_f(gt0, is_ctx):
        for tt in range(4):
            g = gt0 + tt
            pk = next_ps(0, 2)
            for c in range(NCH):
                S.op("pe", lambda e, pk=pk, c=c, tt=tt: e.matmul(ps[pk][0:8, :], lhsT=wfb[:, c, :], rhs=hT_rhs(c, tt),
                                                                start=(c == 0), stop=(c == NCH - 1)),
                     reads=[wfb_b] + hT_bufs(tt), writes=[ps_b[pk]])
            i = g % 2
            S.op("act", lambda e, pk=pk, i=i: e.activation(out=fe[i], in_=ps[pk][0:8, :], func=AF.Exp, scale=-1.0, bias=bfn[:, 0:1]),
                 reads=[ps_b[pk], bfn_b], writes=[fe_b[i]])
            S.op("act", lambda e, i=i: e.activation(out=fe[i], in_=fe[i], func=AF.Ln, bias=1.0, scale=1.0),
                 reads=[fe_b[i]], writes=[fe_b[i]])
            S.op("dve", lambda e, i=i: e.tensor_tensor_scan(out=cst[i], data0=ones8, data1=fe[i], initial=cslast,
                                                            op0=ALU.mult, op1=ALU.add),
                 reads=[fe_b[i], ones8_b, cslast_b], writes=[cst_b[i]])
            S.op("dve", lambda e, i=i: e.tensor_copy(out=cslast, in_=cst[i][:, 511:512]), reads=[cst_b[i]], writes=[cslast_b])
            S.op("pool", lambda e, i=i, g=g: e.tensor_scalar(out=chat[:, g * 512:(g + 1) * 512], in0=cst[i], scalar1=-1.0, scalar2=0.0,
                                                             op0=ALU.mult, op1=ALU.add),
                 reads=[cst_b[i]], writes=[chat_b[g]])
            pr = next_ps(0, 2)
            for k4 in range(4):
                S.op("pe", lambda e, pr=pr, k4=k4, i=i: e.transpose(out=ps[pr][:, k4 * 8:(k4 + 1) * 8], in_=cst[i][:, k4 * 128:(k4 + 1) * 128],
                                                                    identity=ident32[0:8, 0:8]),
                     reads=[cst_b[i], ident32_b], writes=[ps_b[pr]])
            if is_ctx:
                S.op("dve", lambda e, pr=pr, g=g: e.tensor_scalar(out=negc[:, g * 4:(g + 1) * 4, :],
                                                                  in0=ps[pr][:, 0:32].rearrange("p (a b) -> p a b", b=8),
                                                                  scalar1=ctxcol[:, 0:1], scalar2=None, op0=ALU.add),
                     reads=[ps_b[pr], ctxcol_b], writes=[negc_b[g]])
            else:
                S.op("dve", lambda e, pr=pr, g=g: e.tensor_copy(out=negc[:, g * 4:(g + 1) * 4, :],
                                                                in_=ps[pr][:, 0:32].rearrange("p (a b) -> p a b", b=8)),
                     reads=[ps_b[pr]], writes=[negc_b[g]])

    try:
        phase0(0)
        ck(1, [("hT", hT[:].rearrange("p c t -> p (c t)"), hT_b)])
        rope_tables(0)
        ck(2, [("cos", cosT, [cosT_b]), ("sin", sinT, [sinT_b])])
        proj_f(0, True)
        ck(3, [("chat", chat, chat_b), ("negc", negc.rearrange("p t h -> p (t h)"), negc_b)])
        acnt = 0
        for br in range(2):
            for h in range(nheads):
                par = acnt % 2; acnt += 1
                kblk = (8 + h) if br == 0 else (40 + h)
                vblk = (16 + h) if br == 0 else (48 + h)
                wb, wbb = load_weight("main", kblk, NCH, "n")
                dst = lambda tt, par=par: KT[:, par * CT + tt * 512: par * CT + (tt + 1) * 512]
                dstb = (lambda tt: KTctx_b) if par == 0 else (lambda tt: KTown_b[tt])
                if br == 0:
                    proj_fm(wb, wbb, NCH, hT_rhs, hT_bufs, rope_epilogue(dst, dstb, 1.0))
                else:
                    proj_fm(wb, wbb, NCH, hT_rhs, hT_bufs, copy_epilogue(dst, dstb))
                S.dma("sp", KT_s[br, h], KT[:, par * CT:(par + 1) * CT], reads=([KTctx_b] if par == 0 else KTown_b),
                      writes=[KTs_b[br][h]], stream="kts%d" % par, final=(stop is not None))
                wb, wbb = load_weight("main", vblk, NCH, "n")
                proj_v(wb, wbb, par * 16, (lambda g4: Vctx_b) if par == 0 else (lambda g4: Vown_b[g4]))
                S.dma("sp", V_s[br, h], Vf[:, par * 16:(par + 1) * 16, :], reads=([Vctx_b] if par == 0 else Vown_b) + [Vones_b],
                      writes=[Vs_b[br][h]], stream="vs%d" % par, final=(stop is not None))
                if br == 0 and h == 0:
                    ck(4, [("cos", cosT, [cosT_b]), ("sin", sinT, [sinT_b]), ("chat", chat, chat_b),
                           ("negc", negc.rearrange("p t h -> p (t h)"), negc_b)])
        ck(5)

        phase0(CT)
        rope_tables(CT)
        proj_f(4, False)
        ck(6, [("chat", chat, chat_b), ("negc", negc.rearrange("p t h -> p (t h)"), negc_b)])

        octr = [0]
        sctr = [0]
        ptctr = [0]
        onctr = [0]

        def attention(br, h, qi):
            Q = QT[qi]; Qb = QT_b[qi]; Z = zsT[qi]; Zb = zsT_b[qi]
            for qt in range(4):
                oset = octr[0] % 2; octr[0] += 1
                pX, pY = 4 + 2 * oset, 5 + 2 * oset
                nkt = 16 + 4 * (qt + 1)
                def qk(kt):
                    diag = kt >= 16 + 4 * qt
                    o = kt - (16 + 4 * qt) if diag else 0
                    c0 = o * 128
                    pS = 1 + sctr[0] % 3; sctr[0] += 1
                    kbuf = KTctx_b if kt < 16 else KTown_b[(kt - 16) // 4]
                    S.op("pe", lambda e, pS=pS, kt=kt, qt=qt, c0=c0: e.matmul(
                        ps[pS][:, c0:512], lhsT=KT[:, kt * 128:(kt + 1) * 128], rhs=Q[:, qt * 512 + c0:(qt + 1) * 512],
                        start=True, stop=False), reads=[kbuf, Qb[qt]], writes=[ps_b[pS]])
                    if br == 0:
                        S.op("pe", lambda e, pS=pS, kt=kt, qt=qt, c0=c0, diag=diag: e.matmul(
                            ps[pS][:, c0:512], lhsT=Eb[:, kt // 2, :], rhs=selbT_full[:, qt * 512 + c0:(qt + 1) * 512],
                            start=False, stop=(not diag)), reads=[Eb_b, selbT_b[qt]], writes=[ps_b[pS]])
                    else:
                        S.op("pe", lambda e, pS=pS, kt=kt, qt=qt, c0=c0, diag=diag: e.matmul(
                            ps[pS][:, c0:512], lhsT=ohb[:, h, :], rhs=chat_full[:, CT + qt * 512 + c0: CT + (qt + 1) * 512],
                            start=False, stop=(not diag)), reads=[ohb_b, chat_b[4 + qt]], writes=[ps_b[pS]])
                    if diag:
                        S.op("pe", lambda e, pS=pS, o=o, c0=c0: e.matmul(
                            ps[pS][:, c0:512], lhsT=identb[:], rhs=cmaskb[:, o, c0:512], start=False, stop=True),
                            reads=[identb_b, cmaskb_b], writes=[ps_b[pS]])
                    return (pS, o, c0)

                def expv(kt, info):
                    pS, o, c0 = info
                    vbuf = Vctx_b if kt < 16 else Vown_b[(kt - 16) // 4]
                    pi = ptctr[0] % 3; ptctr[0] += 1
                    if br == 0:
                        S.op("act", lambda e, pS=pS, pi=pi, c0=c0: e.activation(out=PT[pi][:, c0:512], in_=ps[pS][:, c0:512], func=AF.Exp),
                             reads=[ps_b[pS]], writes=[PT_b[pi]])
                    else:
                        S.op("act", lambda e, pS=pS, pi=pi, c0=c0, kt=kt: e.activation(
                            out=PT[pi][:, c0:512], in_=ps[pS][:, c0:512], func=AF.Exp, bias=negc[:, kt, h:h + 1], scale=1.0),
                            reads=[ps_b[pS], negc_b[kt // 4]], writes=[PT_b[pi]])
                    for j in range(o, 4):
                        pO = pX if j < 2 else pY
                        col = (j % 2) * 256
                        last = (kt == 16 + 4 * qt + j)
                        S.op("pe", lambda e, pO=pO, col=col, pi=pi, j=j, kt=kt, last=last: e.matmul(
                            ps[pO][:, col:col + 129], lhsT=PT[pi][:, j * 128:(j + 1) * 128], rhs=Vf[:, kt, :],
                            start=(kt == 0 and j % 2 == 0), stop=last, skip_group_check=True),
                            reads=[PT_b[pi], vbuf, Vones_b], writes=[ps_b[pO]])

                infos = {0: qk(0), 1: qk(1)}
                for kt in range(nkt):
                    if kt + 2 < nkt:
                        infos[kt + 2] = qk(kt + 2)
                    expv(kt, infos.pop(kt))
                gi = onctr[0] % 2; onctr[0] += 1
                pT_ = next_ps(0, 2)
                for j in range(4):
                    pO = pX if j < 2 else pY
                    col = (j % 2) * 256
                    oi = j % 2
                    S.op("dve", lambda e, pO=pO, col=col, oi=oi: e.reciprocal(out=rden[oi], in_=ps[pO][:, col + 128:col + 129]),
                         reads=[ps_b[pO]], writes=[rden_b[oi]])
                    S.op("dve", lambda e, pO=pO, col=col, oi=oi: e.tensor_scalar(out=On[oi], in0=ps[pO][:, col:col + 128], scalar1=rden[oi],
                                                                                 scalar2=None, op0=ALU.mult),
                         reads=[ps_b[pO], rden_b[oi]], writes=[On_b[oi]])
                    S.op("pe", lambda e, pT_=pT_, j=j, oi=oi: e.transpose(out=ps[pT_][:].bitcast(BF16)[:, j * 128:(j + 1) * 128], in_=On[oi],
                                                                          identity=identb[:]),
                         reads=[On_b[oi], identb_b], writes=[ps_b[pT_]])
                S.op("dve", lambda e, pT_=pT_, gi=gi, qt=qt: e.tensor_tensor(out=GTt[gi], in0=ps[pT_][:].bitcast(BF16)[:, 0:512],
                                                                             in1=Z[:, qt * 512:(qt + 1) * 512], op=ALU.mult),
                     reads=[ps_b[pT_], Zb[qt]], writes=[GTt_b[gi]])
                S.dma("sp", GT_s[:, br * 8 + h, qt * 512:(qt + 1) * 512], GTt[gi], reads=[GTt_b[gi]], writes=[GTs_b[br * 8 + h][qt]], stream="gts%d" % gi, final=(stop is not None))

        def moba_select(qi):
            Q = QT[qi]; Qb = QT_b[qi]
            S.op("dve", lambda e: e.tensor_reduce(out=kbar32, in_=KT.rearrange("p (b k) -> p b k", k=256), axis=AX.X, op=ALU.add),
                 reads=[KTctx_b] + KTown_b, writes=[kbar32_b])
            S.op("dve", lambda e: e.tensor_scalar(out=kbarb, in0=kbar32, scalar1=1.0 / 256, scalar2=None, op0=ALU.mult),
                 reads=[kbar32_b], writes=[kbarb_b])
            pg = next_ps(0, 2)
            for j in range(16):
                S.op("pe", lambda e, j=j: e.matmul(ps[pg][:, j * 16:(j + 1) * 16], lhsT=Q[:, j * 128:(j + 1) * 128], rhs=kbarb,
                                                   start=(j == 0), stop=True, skip_group_check=True),
                     reads=[Qb[j // 4], kbarb_b], writes=[ps_b[pg]])
            S.op("dve", lambda e: e.tensor_tensor(out=gsb, in0=ps[pg][:, 0:256], in1=ctxneg[:], op=ALU.add),
                 reads=[ps_b[pg], ctxneg_b], writes=[gsb_b])
            for j in range(16):
                npast = 8 + j // 2
                S.op("dve", lambda e, j=j, npast=npast: e.max(out=m8[:, j * 8:(j + 1) * 8], in_=gsb[:, j * 16:j * 16 + npast]),
                     reads=[gsb_b], writes=[m8_b])
            for j in range(16):
                S.op("dve", lambda e, j=j: e.tensor_scalar(out=selb[:, j * 16:(j + 1) * 16], in0=gsb[:, j * 16:(j + 1) * 16],
                                                           scalar1=m8[:, j * 8 + 2:j * 8 + 3], scalar2=NEG, op0=ALU.is_lt, op1=ALU.mult),
                     reads=[gsb_b, m8_b], writes=[selb_b])
            S.op("dve", lambda e: e.tensor_tensor(out=selb, in0=selb, in1=ctx30[:], op=ALU.add), reads=[selb_b, ctx30_b], writes=[selb_b])
            S.op("dve", lambda e: e.tensor_tensor(out=selb, in0=selb, in1=pastm[:], op=ALU.mult), reads=[selb_b, pastm_b], writes=[selb_b])
        def moba_select2():
            for qt in range(4):
                pk = next_ps(0, 2)
                for j4 in range(4):
                    j = qt * 4 + j4
                    S.op("pe", lambda e, pk=pk, j=j, j4=j4: e.transpose(out=ps[pk][0:16, j4 * 128:(j4 + 1) * 128], in_=selb[:, j * 16:(j + 1) * 16],
                                                                        identity=ident32[:]),
                         reads=[selb_b, ident32_b], writes=[ps_b[pk]])
                S.op("act", lambda e, pk=pk, qt=qt: e.activation(out=selbT[:, qt * 512:(qt + 1) * 512], in_=ps[pk][0:16, :], func=AF.Copy),
                     reads=[ps_b[pk]], writes=[selbT_b[qt]])

        hctr = 0
        for br in range(2):
            for h in range(nheads):
                qi = hctr % 2; hctr += 1
                base = 0 if br == 0 else 32
                qblk, kblk, vblk, zblk = base + h, base + 8 + h, base + 16 + h, base + 24 + h
                S.dma("sp", KT[:, 0:CT], KT_s[br, h], reads=[KTs_b[br][h]], writes=[KTctx_b], stream="ktl")
                S.dma("sp", Vf[:, 0:16, :], V_s[br, h], reads=[Vs_b[br][h]], writes=[Vctx_b], stream="vl")
                wb, wbb = load_weight("main", kblk, NCH, "n")
                dst = lambda tt: KT[:, CT + tt * 512: CT + (tt + 1) * 512]
                dstb = lambda tt: KTown_b[tt]
                if br == 0:
                    proj_fm(wb, wbb, NCH, hT_rhs, hT_bufs, rope_epilogue(dst, dstb, 1.0))
                else:
                    proj_fm(wb, wbb, NCH, hT_rhs, hT_bufs, copy_epilogue(dst, dstb))
                wb, wbb = load_weight("main", qblk, NCH, "n")
                dstq = lambda tt, qi=qi: QT[qi][:, tt * 512:(tt + 1) * 512]
                dstqb = lambda tt, qi=qi: QT_b[qi][tt]
                if br == 0:
                    proj_fm(wb, wbb, NCH, hT_rhs, hT_bufs, rope_epilogue(dstq, dstqb, 1.0 / math.sqrt(HD)))
                else:
                    proj_fm(wb, wbb, NCH, hT_rhs, hT_bufs, copy_epilogue(dstq, dstqb, scale=1.0 / math.sqrt(HD)))
                if br == 0:
                    moba_select(qi)
                wb, wbb = load_weight("main", zblk, NCH, "n")
                dstz = lambda tt, qi=qi: zsT[qi][:, tt * 512:(tt + 1) * 512]
                dstzb = lambda tt, qi=qi: zsT_b[qi][tt]
                proj_fm(wb, wbb, NCH, hT_rhs, hT_bufs, copy_epilogue(dstz, dstzb, func=AF.Silu))
                wb, wbb = load_weight("main", vblk, NCH, "n")
                proj_v(wb, wbb, 16, lambda g4: Vown_b[g4])
                if br == 0:
                    moba_select2()
                    if h == 0:
                        ck(7, [("selbT", selbT, selbT_b), ("QT", QT[qi], QT_b[qi]), ("KT", KT, [KTctx_b] + KTown_b),
                               ("zsT", zsT[qi], zsT_b[qi]), ("V", Vf.rearrange("p t d -> p (t d)"), [Vctx_b, Vones_b] + Vown_b)])
                attention(br, h, qi)
                if br == 0 and h == 0:
                    ck(8)
                if br == 1 and h == 0:
                    ck(9)

        for blk in range(32):
            wb, wbb = load_weight("main", 64 + blk, NCH, "n")
            gi2 = blk % 2
            proj_fm(wb, wbb, NCH, hT_rhs, hT_bufs,
                    copy_epilogue(lambda tt, gi2=gi2: QT[gi2][:, tt * 512:(tt + 1) * 512], lambda tt, gi2=gi2: QT_b[gi2][tt], func=AF.Sigmoid))
            S.dma("act", SG_s[blk], QT[gi2], reads=QT_b[gi2], writes=[SGs_b[blk]], stream="sgs%d" % gi2)

        S.barrier()
        ar_off[0] = 0
        GTl = bfv(carve(4096)).rearrange("p (c t) -> p c t", t=512); GTl_b = [Buf("GTl%d" % i) for i in range(16)]
        mixT = bfv(carve(4096)).rearrange("p (c t) -> p c t", t=512); mixT_b = [Buf("mixT%d" % i) for i in range(16)]
        xT = carve(8192).rearrange("p (c t) -> p c t", t=512); xT_b = [Buf("xT%d" % i) for i in range(16)]
        pT = bfv(carve(512)).rearrange("p (c t) -> p c t", t=512); pT_b = Buf("pT")
        gfin = carve(2048); gfin_b = Buf("gfin")
        sg = [bfv(carve(256)) for _ in range(4)]; sg_b = [Buf("sg%d" % i) for i in range(4)]
        tm = [carve(512) for _ in range(4)]; tm_b = [Buf("tm%d" % i) for i in range(4)]
        rbc = carve(512); rbc_b = Buf("rbc")
        ssq3 = carve(4); ssq3_b = Buf("ssq3")
        rs3 = carve(1); rs3_b = Buf("rs3")
        ptmp = carve(256); ptmp_b = Buf("ptmp")
        assert ar_off[0] <= AR_W

        S.dma("sp", gfin, g_fin, writes=[gfin_b], stream="c_gfin")

        tmc = [0]
        for tt in range(4):
            t0 = tt * 512
            S.dma("sp", GTl, GT_s[:, :, t0:t0 + 512], reads=[GTs_b[hb][tt] for hb in range(16)], writes=GTl_b, stream="gtl")
            for s4 in range(4):
                i = wctr[0]; wctr[0] += 1
                st = stage[i % 3]; stb = stage_b[i % 3]
                r0 = CT + t0 + s4 * 128
                S.dma("sp", st[:], x_all[r0:r0 + 128, :], writes=[stb], stream="st%d" % (i % 3))
                for b4 in range(4):
                    pk = next_ps(0, 4)
                    for c4 in range(4):
                        c = b4 * 4 + c4
                        S.op("pe", lambda e, pk=pk, c4=c4, c=c, st=st: e.transpose(out=ps[pk][:, c4 * 128:(c4 + 1) * 128],
                                                                                  in_=st[:, c * 128:(c + 1) * 128], identity=ident32[:]),
                             reads=[stb, ident32_b], writes=[ps_b[pk]])
                    S.op("act" if b4 % 2 == 0 else "dve",
                         (lambda e, pk=pk, b4=b4, s4=s4: e.activation(out=xT[:, b4 * 4:(b4 + 1) * 4, s4 * 128:(s4 + 1) * 128],
                                                                      in_=ps[pk][:].rearrange("p (a b) -> p a b", b=128), func=AF.Copy))
                         if b4 % 2 == 0 else
                         (lambda e, pk=pk, b4=b4, s4=s4: e.tensor_copy(out=xT[:, b4 * 4:(b4 + 1) * 4, s4 * 128:(s4 + 1) * 128],
                                                                       in_=ps[pk][:].rearrange("p (a b) -> p a b", b=128))),
                         reads=[ps_b[pk]], writes=xT_b[b4 * 4:(b4 + 1) * 4])
            for s4 in range(4):
                S.dma("sp", ptmp, p_own[t0 + s4 * 128: t0 + (s4 + 1) * 128, :], writes=[ptmp_b], stream="ptmp")
                pk = next_ps(0, 4)
                for c2 in range(2):
                    S.op("pe", lambda e, pk=pk, c2=c2: e.transpose(out=ps[pk][:, c2 * 128:(c2 + 1) * 128],
                                                                   in_=ptmp[:, c2 * 128:(c2 + 1) * 128], identity=ident32[:]),
                         reads=[ptmp_b, ident32_b], writes=[ps_b[pk]])
                S.op("dve", lambda e, pk=pk, s4=s4: e.tensor_copy(out=pT[:, :, s4 * 128:(s4 + 1) * 128],
                                                                  in_=ps[pk][:, 0:256].rearrange("p (a b) -> p a b", b=128)),
                     reads=[ps_b[pk]], writes=[pT_b])

            if tt == ctile:
                ck(10, [("xT", xT.rearrange("p c t -> p (c t)"), xT_b), ("pT", pT.rearrange("p c t -> p (c t)"), [pT_b])])
            for n in range(16):
                res = []
                for which in range(2):
                    si = (2 * n + which) % 4
                    S.dma("sp", sg[si], SG_s[which * 16 + n][:, t0:t0 + 512], reads=[SGs_b[which * 16 + n]], writes=[sg_b[si]],
                          stream="sgl%d" % si)
                res = [None, None]
                for which in (2, 3):
                    if which == 2:
                        wb, wbb = load_weight("ba", n, 8, "-", tt)
                        nch, rf, rb = 8, (lambda c: GTl[:, c, :]), GTl_b[0:8]
                    else:
                        wb, wbb = load_weight("bb", n, 8, "-", tt)
                        nch, rf, rb = 8, (lambda c: GTl[:, 8 + c, :]), GTl_b[8:16]
                    pk = next_ps(0, 8)
                    for c in range(nch):
                        S.op("pe", lambda e, pk=pk, c=c, wb=wb, rf=rf, nch=nch: e.matmul(ps[pk][:], lhsT=wb[:, c, :], rhs=rf(c),
                                                                                        start=(c == 0), stop=(c == nch - 1)),
                             reads=[wbb] + list(rb), writes=[ps_b[pk]])
                    res.append(pk)
                sA, sB = (2 * n) % 4, (2 * n + 1) % 4
                ta = tmc[0] % 4; tb = (tmc[0] + 1) % 4; tmc[0] += 2
                S.op("dve", lambda e, ta=ta, sA=sA, pk=res[2]: e.tensor_tensor(out=tm[ta], in0=ps[pk][:], in1=sg[sA], op=ALU.mult),
                     reads=[ps_b[res[2]], sg_b[sA]], writes=[tm_b[ta]])
                S.op("dve", lambda e, tb=tb, sB=sB, pk=res[3]: e.tensor_tensor(out=tm[tb], in0=ps[pk][:], in1=sg[sB], op=ALU.mult),
                     reads=[ps_b[res[3]], sg_b[sB]], writes=[tm_b[tb]])
                S.op("pool", lambda e, ta=ta, tb=tb, n=n: e.tensor_tensor(out=mixT[:, n, :], in0=tm[ta], in1=tm[tb], op=ALU.add),
                     reads=[tm_b[ta], tm_b[tb]], writes=[mixT_b[n]])

            if tt == ctile:
                ck(11, [("mixT", mixT.rearrange("p c t -> p (c t)"), mixT_b)])
            for m in range(16):
                wb, wbb = load_weight("o", m, NCH, "-", tt)
                pk = next_ps(0, 8)
                for c in range(NCH):
                    S.op("pe", lambda e, pk=pk, c=c, wb=wb: e.matmul(ps[pk][:], lhsT=wb[:, c, :], rhs=mixT[:, c, :],
                                                                    start=(c == 0), stop=(c == NCH - 1)),
                         reads=[wbb, mixT_b[c]], writes=[ps_b[pk]])
                S.op("dve", lambda e, pk=pk, m=m: e.tensor_tensor(out=xT[:, m, :], in0=ps[pk][:], in1=xT[:, m, :], op=ALU.add),
                     reads=[ps_b[pk], xT_b[m]], writes=[xT_b[m]])

            if tt == ctile:
                ck(12, [("x2T", xT.rearrange("p c t -> p (c t)"), xT_b)])
            pq = 5
            for c in range(NCH):
                ti = tmc[0] % 4; tmc[0] += 1
                S.op("act", lambda e, ti=ti, c=c: e.activation(out=tm[ti], in_=xT[:, c, :], func=AF.Square),
                     reads=[xT_b[c]], writes=[tm_b[ti]])
                S.op("pe", lambda e, ti=ti, c=c: e.matmul(ps[pq][:], lhsT=ones32[:], rhs=tm[ti], start=(c == 0), stop=(c == NCH - 1)),
                     reads=[ones32_b, tm_b[ti]], writes=[ps_b[pq]])
            S.op("act", lambda e: e.activation(out=rbc, in_=ps[pq][:], func=AF.Sqrt, scale=1.0 / D, bias=1e-6),
                 reads=[ps_b[pq]], writes=[rbc_b])
            S.op("dve", lambda e: e.reciprocal(out=rbc, in_=rbc), reads=[rbc_b], writes=[rbc_b])
            for c in range(NCH):
                S.op("pool" if c % 2 == 0 else "dve",
                     lambda e, c=c: e.tensor_tensor(out=GTl[:, c, :], in0=xT[:, c, :], in1=rbc, op=ALU.mult),
                     reads=[xT_b[c], rbc_b], writes=[GTl_b[c]])

            if tt == ctile:
                ck(13, [("h2T", GTl.rearrange("p c t -> p (c t)"), GTl_b)])
            for m in range(16):
                wb, wbb = load_weight("pg", m, NCH, "p", tt)
                pk = next_ps(0, 8)
                for c in range(NCH):
                    S.op("pe", lambda e, pk=pk, c=c, wb=wb: e.matmul(ps[pk][:], lhsT=wb[:, c, :], rhs=GTl[:, c, :],
                                                                    start=(c == 0), stop=(c == NCH - 1)),
                         reads=[wbb, GTl_b[c]], writes=[ps_b[pk]])
                si = m % 4
                S.op("act", lambda e, pk=pk, si=si: e.activation(out=sg[si], in_=ps[pk][:], func=AF.Sigmoid),
                     reads=[ps_b[pk]], writes=[sg_b[si]])
                wb2, wbb2 = load_weight("up", m, 2, "-", tt)
                pk2 = next_ps(0, 8)
                for c in range(2):
                    S.op("pe", lambda e, pk2=pk2, c=c, wb2=wb2: e.matmul(ps[pk2][:], lhsT=wb2[:, c, :], rhs=pT[:, c, :],
                                                                        start=(c == 0), stop=(c == 1)),
                         reads=[wbb2, pT_b], writes=[ps_b[pk2]])
                ti = tmc[0] % 4; tmc[0] += 1
                S.op("dve", lambda e, pk2=pk2, si=si, ti=ti: e.tensor_tensor(out=tm[ti], in0=ps[pk2][:], in1=sg[si], op=ALU.mult),
                     reads=[ps_b[pk2], sg_b[si]], writes=[tm_b[ti]])
                S.op("pool", lambda e, ti=ti, m=m: e.tensor_tensor(out=xT[:, m, :], in0=xT[:, m, :], in1=tm[ti], op=ALU.add),
                     reads=[tm_b[ti], xT_b[m]], writes=[xT_b[m]])

            if tt == ctile:
                ck(14, [("x3T", xT.rearrange("p c t -> p (c t)"), xT_b), ("mixT", mixT.rearrange("p c t -> p (c t)"), mixT_b), ("h2T", GTl.rearrange("p c t -> p (c t)"), GTl_b), ("pT", pT.rearrange("p c t -> p (c t)"), [pT_b])])
            for s4 in range(4):
                pks = [4 * (s4 % 2) + b for b in range(4)]
                for b4 in range(4):
                    for c4 in range(4):
                        c = b4 * 4 + c4
                        S.op("pe", lambda e, pk=pks[b4], c4=c4, c=c, s4=s4: e.transpose(out=ps[pk][:, c4 * 128:(c4 + 1) * 128],
                                                                                       in_=xT[:, c, s4 * 128:(s4 + 1) * 128], identity=ident32[:]),
                             reads=[xT_b[c], ident32_b], writes=[ps_b[pks[b4]]])
                i = wctr[0]; wctr[0] += 1
                st = stage[i % 3]; stb = stage_b[i % 3]
                for b4 in range(4):
                    S.op("act", lambda e, pk=pks[b4], b4=b4, st=st: e.activation(out=st[:, b4 * 512:(b4 + 1) * 512], in_=ps[pk][:], func=AF.Square,
                                                                                accum_out=ssq3[:, b4:b4 + 1]),
                         reads=[ps_b[pks[b4]]], writes=[stb, ssq3_b])
                S.op("dve", lambda e: e.tensor_reduce(out=rs3, in_=ssq3, axis=AX.X, op=ALU.add), reads=[ssq3_b], writes=[rs3_b])
                S.op("act", lambda e: e.activation(out=rs3, in_=rs3, func=AF.Sqrt, scale=1.0 / D, bias=1e-6), reads=[rs3_b], writes=[rs3_b])
                S.op("dve", lambda e: e.reciprocal(out=rs3, in_=rs3), reads=[rs3_b], writes=[rs3_b])
                for b4 in range(4):
                    S.op("dve", lambda e, pk=pks[b4], b4=b4, st=st: e.scalar_tensor_tensor(
                        out=st[:, b4 * 512:(b4 + 1) * 512], in0=ps[pk][:], scalar=rs3, in1=gfin[:, b4 * 512:(b4 + 1) * 512],
                        op0=ALU.mult, op1=ALU.mult), reads=[ps_b[pks[b4]], rs3_b, gfin_b], writes=[stb])
                r0 = t0 + s4 * 128
                S.dma("sp", out_d[r0:r0 + 128, :], st[:], reads=[stb], writes=[], stream="st%d" % (i % 3), final=True)
            if tt == ctile:
                ck(15)


    except _Stop:
        pass
    if rec is None:
        S.emit()
    es.close()
    return nc


def _blk(w, nchunk):
    K, N = w.shape
    return np.ascontiguousarray(w.reshape(nchunk, 128, N // 128, 128).transpose(2, 1, 0, 3))


def _consts(half):
    ident = np.eye(128, dtype=np.float32)
    rotT = np.zeros((128, 128), np.float32)
    for m in range(64):
        rotT[m + 64, m] = -1.0
    for m in range(64, 128):
        rotT[m - 64, m] = 1.0
    j = np.arange(128) % 64
    inv = (10000.0 ** (-(2.0 * j) / 128.0)).astype(np.float32).reshape(128, 1)
    cmask = np.zeros((128, 4, 512), np.float32)
    s = np.arange(128)[:, None]
    t = np.arange(512)[None, :]
    for o in range(4):
        cmask[:, o, :] = np.where(o * 128 + s <= t, 0.0, NEG)
    E = np.zeros((16, 16, 128), np.float32)
    for i in range(16):
        E[i, i, :] = 1.0
    oh = np.zeros((8, 8, 128), np.float32)
    for i in range(8):
        oh[i, i, :] = 1.0
    ctxneg = np.zeros((128, 16, 16), np.float32)
    ctx30 = np.zeros((128, 16, 16), np.float32)
    past = np.zeros((128, 16, 16), np.float32)
    for jj in range(16):
        past[:, jj, 0:8 + jj // 2] = 1.0
    if half == 0:
        ctxneg[:, :, 0:8] = -1e30
        ctx30[:, :, 0:8] = NEG
    ctxcol = np.full((128, 1), NEG if half == 0 else 0.0, np.float32)
    return {"c_ident": ident, "c_rotT": rotT, "c_inv": inv, "c_cmask": cmask, "c_E": E, "c_oh": oh,
            "c_ctxneg": ctxneg.reshape(128, 256), "c_ctx30": ctx30.reshape(128, 256),
            "c_past": past.reshape(128, 256), "c_ctxcol": ctxcol}


_NC_CACHE = {}


def make_in_maps(x, p, positions, g_norm, w_in, b_f, w_branch_a, w_branch_b, w_out, g_ple, w_ple_gate, w_ple_up, g_final):
    x = np.asarray(x, np.float32); p = np.asarray(p, np.float32); positions = np.asarray(positions, np.int32)
    w_in = np.asarray(w_in, np.float32)[0]
    wm = np.concatenate([w_in[:, 0:7168], w_in[:, 7176:12296]], axis=1)
    shared = {
        "w_main": _blk(wm, 16),
        "w_f": np.ascontiguousarray(w_in[:, 7168:7176].reshape(16, 128, 8).transpose(1, 0, 2)),
        "w_ba": _blk(np.asarray(w_branch_a, np.float32)[0], 8),
        "w_bb": _blk(np.asarray(w_branch_b, np.float32)[0], 8),
        "w_o": _blk(np.asarray(w_out, np.float32)[0], 16),
        "w_pg": _blk(np.asarray(w_ple_gate, np.float32)[0], 16),
        "w_up": _blk(np.asarray(w_ple_up, np.float32)[0], 2),
        "g_norm": np.ascontiguousarray(np.asarray(g_norm, np.float32)[0].reshape(16, 128).T),
        "g_ple": np.ascontiguousarray(np.asarray(g_ple, np.float32)[0].reshape(16, 128).T),
        "g_fin": np.ascontiguousarray(np.broadcast_to(np.asarray(g_final, np.float32)[None, :], (128, D))),
        "b_f": np.asarray(b_f, np.float32)[0].reshape(8, 1),
    }
    maps = []
    for core in range(8):
        b, half = core // 2, core % 2
        m = dict(shared)
        m["x_all"] = np.ascontiguousarray(np.concatenate([x[b, 0:CT], x[b, half * T:(half + 1) * T]], axis=0))
        m["pos_all"] = np.ascontiguousarray(np.concatenate([positions[b, 0:CT], positions[b, half * T:(half + 1) * T]])[None, :])
        m["p_own"] = np.ascontiguousarray(p[0, b, half * T:(half + 1) * T])
        m.update(_consts(half))
        maps.append(m)
    return maps


def kernel(x, p, positions, g_norm, w_in, b_f, w_branch_a, w_branch_b, w_out, g_ple, w_ple_gate, w_ple_up, g_final):
    maps = make_in_maps(x, p, positions, g_norm, w_in, b_f, w_branch_a, w_branch_b, w_out, g_ple, w_ple_gate, w_ple_up, g_final)
    if "nc" not in _NC_CACHE:
        rec = []
        build_program(rec=rec)
        _NC_CACHE["nc"] = build_program(wseq=rec)
    nc = _NC_CACHE["nc"]
    res = run_bass_kernel_spmd(nc, maps, core_ids=list(range(8)))
    out = np.zeros((4, 2 * T, D), np.float32)
    for core in range(8):
        b, half = core // 2, core % 2
        out[b, half * T:(half + 1) * T] = res.results[core]["out"]
    return out
```

```python
import math
from contextlib import ExitStack

import numpy as np
import concourse.bass as bass
import concourse.mybir as mybir
from concourse.bass_utils import run_bass_kernel_spmd

F32 = mybir.dt.float32
BF16 = mybir.dt.bfloat16
I32 = mybir.dt.int32
AF = mybir.ActivationFunctionType
ALU = mybir.AluOpType
AX = mybir.AxisListType

D = 2048
NCH = 16
T = 2048
CT = 2048
HD = 128
NH = 8
NEG = -30000.0
MAGIC = 12582912.0
TWO_PI = 2.0 * math.pi
C1 = 6.28125
C2 = TWO_PI - C1
PI_SAFE = 3.1415925


class Buf:
    __slots__ = ("name", "w", "r")

    def __init__(self, name):
        self.name = name
        self.w = None
        self.r = []


class Op:
    __slots__ = ("eng", "fn", "deps", "signal", "tok_sem", "tok_val", "is_dma", "final", "seq")

    def __init__(self, eng, fn, is_dma=False):
        self.eng = eng
        self.fn = fn
        self.deps = []
        self.signal = False
        self.tok_sem = None
        self.tok_val = None
        self.is_dma = is_dma
        self.final = False


class Sched:
    ENGS = ("pe", "act", "dve", "pool", "sp")

    def __init__(self, nc):
        self.nc = nc
        self.ops = {e: [] for e in self.ENGS}
        self.streams = {}
        self.finals = []
        self.bar_ops = []
        self.bar_taken = set(self.ENGS)
        self.dma_since_bar = []
        self.seq = 0

    def _add_deps(self, op, reads, writes):
        self.seq += 1
        op.seq = self.seq
        cands = []
        for b in reads:
            if b.w is not None:
                cands.append((b.w, True))
        for b in writes:
            if b.w is not None:
                cands.append((b.w, True))
            for r in b.r:
                cands.append((r, False))
        if op.eng not in self.bar_taken:
            self.bar_taken.add(op.eng)
            for d in self.bar_ops:
                cands.append((d, True))
        best = {}
        for d, raw in cands:
            if d is op:
                continue
            if (not d.is_dma) and (not op.is_dma) and d.eng == op.eng:
                if op.eng == "pe" or not raw:
                    continue
            key = d.tok_sem if d.is_dma else ("eng", d.eng)
            cur = best.get(key)
            if cur is None or d.seq > cur.seq:
                best[key] = d
        for d in best.values():
            op.deps.append(d)
            d.signal = True
        for b in reads:
            b.r.append(op)
        for b in writes:
            b.w = op
            b.r = []

    def op(self, eng, fn, reads=(), writes=()):
        o = Op(eng, fn)
        self._add_deps(o, list(reads), list(writes))
        self.ops[eng].append(o)
        return o

    def dma(self, queue, out, in_, reads=(), writes=(), stream=None, final=False):
        o = Op(queue, (out, in_), is_dma=True)
        o.signal = True
        n = self.streams.get(stream, 0) + 1
        self.streams[stream] = n
        o.tok_sem = stream
        o.tok_val = 16 * n
        o.final = final
        self._add_deps(o, list(reads), list(writes))
        self.ops[queue].append(o)
        self.dma_since_bar.append(o)
        if final:
            self.finals.append(o)
        return o

    def barrier(self):
        last = {}
        for o in self.dma_since_bar:
            last[o.tok_sem] = o
        bar = list(last.values())
        for e in self.ENGS:
            for o in reversed(self.ops[e]):
                if not o.is_dma:
                    bar.append(o)
                    break
        self.bar_ops = bar
        self.bar_taken = set()
        self.dma_since_bar = []

    def emit(self):
        nc = self.nc
        for e in self.ENGS:
            cnt = 0
            for o in self.ops[e]:
                if o.is_dma:
                    continue
                if o.signal:
                    cnt += 1
                    o.tok_sem = "eng_" + e
                    o.tok_val = cnt
        with ExitStack() as es:
            sems = {}
            for e in self.ENGS:
                sems["eng_" + e] = es.enter_context(nc.semaphore("s_" + e))
            for s in self.streams:
                sems[s] = es.enter_context(nc.semaphore("d_" + s))
            block = es.enter_context(nc.Block())
            handles = {"pe": block.tensor, "act": block.scalar, "dve": block.vector,
                       "pool": block.gpsimd, "sp": block.sync}
            finals = self.finals

            def make(e):
                def body(eng):
                    known = {}
                    for o in self.ops[e]:
                        for d in o.deps:
                            if known.get(d.tok_sem, 0) < d.tok_val:
                                eng.wait_ge(sems[d.tok_sem], d.tok_val)
                                known[d.tok_sem] = d.tok_val
                        if o.is_dma:
                            out, in_ = o.fn
                            eng.dma_start(out=out, in_=in_).then_inc(sems[o.tok_sem], 16)
                        else:
                            ins = o.fn(eng)
                            if o.signal:
                                ins.then_inc(sems[o.tok_sem], 1)
                    if e == "sp":
                        for o in finals:
                            if known.get(o.tok_sem, 0) < o.tok_val:
                                eng.wait_ge(sems[o.tok_sem], o.tok_val)
                                known[o.tok_sem] = o.tok_val
                return body
            for e in self.ENGS:
                handles[e](make(e))


class _Stop(Exception):
    pass


def build_program(dbg=False, stop=None, nheads=NH, ctile=0, wseq=None, rec=None):
    nc = bass.Bass("TRN2", target_bir_lowering=False)
    dumps = []

    def ck(k, items=()):
        if stop == k:
            for name, ap, bufs in items:
                d = nc.dram_tensor("dbg_" + name, list(ap.shape), ap.dtype, kind="ExternalOutput").ap()
                S.dma("sp", d, ap, reads=bufs, writes=[], stream="dbg_" + name, final=True)
            raise _Stop()

    def din(name, shape, dt=F32):
        return nc.dram_tensor(name, list(shape), dt, kind="ExternalInput").ap()

    x_all = din("x_all", [CT + T, D])
    pos_all = din("pos_all", [1, CT + T], I32)
    p_own = din("p_own", [T, 256])
    w_main = din("w_main", [96, 128, NCH, 128])
    w_f = din("w_f", [128, NCH, 8])
    w_ba = din("w_ba", [16, 128, 8, 128])
    w_bb = din("w_bb", [16, 128, 8, 128])
    w_o = din("w_o", [16, 128, NCH, 128])
    w_pg = din("w_pg", [16, 128, NCH, 128])
    w_up = din("w_up", [16, 128, 2, 128])
    g_norm = din("g_norm", [128, NCH])
    g_ple = din("g_ple", [128, NCH])
    g_fin = din("g_fin", [128, D])
    b_f = din("b_f", [8, 1])
    c_ident = din("c_ident", [128, 128])
    c_rotT = din("c_rotT", [128, 128])
    c_inv = din("c_inv", [128, 1])
    c_cmask = din("c_cmask", [128, 4, 512])
    c_E = din("c_E", [16, 16, 128])
    c_oh = din("c_oh", [8, 8, 128])
    c_ctxneg = din("c_ctxneg", [128, 256])
    c_ctx30 = din("c_ctx30", [128, 256])
    c_past = din("c_past", [128, 256])
    c_ctxcol = din("c_ctxcol", [128, 1])
    out_d = nc.dram_tensor("out", [T, D], F32, kind="ExternalOutput").ap()

    skind = "ExternalOutput" if dbg else "Internal"
    KT_s = nc.dram_tensor("KT_s", [2, NH, 128, CT], BF16, kind=skind).ap()
    V_s = nc.dram_tensor("V_s", [2, NH, 128, 16, 129], BF16, kind=skind).ap()
    GT_s = nc.dram_tensor("GT_s", [128, 16, T], BF16, kind=skind).ap()
    SG_s = nc.dram_tensor("SG_s", [32, 128, T], BF16, kind=skind).ap()
    WS = {"ba": nc.dram_tensor("WS_ba", [16, 128, 1024], BF16, kind=skind).ap(),
          "bb": nc.dram_tensor("WS_bb", [16, 128, 1024], BF16, kind=skind).ap(),
          "o": nc.dram_tensor("WS_o", [16, 128, 2048], BF16, kind=skind).ap(),
          "pg": nc.dram_tensor("WS_pg", [16, 128, 2048], BF16, kind=skind).ap(),
          "up": nc.dram_tensor("WS_up", [16, 128, 256], BF16, kind=skind).ap()}

    S = Sched(nc)
    es = ExitStack()
    KTs_b = [[Buf("KTs") for _ in range(NH)] for _ in range(2)]
    Vs_b = [[Buf("Vs") for _ in range(NH)] for _ in range(2)]
    GTs_b = [[Buf("GTs") for _ in range(4)] for _ in range(16)]
    SGs_b = [Buf("SGs") for _ in range(32)]
    WSb = {k: [Buf("WS" + k) for _ in range(16)] for k in ("ba", "bb", "o", "pg", "up")}

    def sb(name, shape, dt):
        return es.enter_context(nc.sbuf_tensor(name, list(shape), dt))

    hT = sb("hT", [128, NCH, T], BF16)
    hT_b = [Buf("hT%d" % i) for i in range(16)]
    stage = [sb("stage%d" % i, [128, D], F32) for i in range(3)]
    stage_b = [Buf("stage%d" % i) for i in range(3)]
    wbf = [sb("wbf%d" % i, [128, NCH, 128], BF16) for i in range(3)]
    wbf_b = [Buf("wbf%d" % i) for i in range(3)]
    ident32 = sb("ident32", [128, 128], F32); ident32_b = Buf("ident32")
    identb = sb("identb", [128, 128], BF16); identb_b = Buf("identb")
    rotT = sb("rotT", [128, 128], F32); rotT_b = Buf("rotT")
    ones32 = sb("ones32", [128, 128], F32); ones32_b = Buf("ones32")
    cmaskb = sb("cmaskb", [128, 4, 512], BF16); cmaskb_b = Buf("cmaskb")
    Eb = sb("Eb", [128, 16, 128], BF16); Eb_b = Buf("Eb")
    ohb = sb("ohb", [128, 8, 128], BF16); ohb_b = Buf("ohb")
    gn = sb("gn", [128, NCH], F32); gn_b = Buf("gn")
    gp = sb("gp", [128, NCH], F32); gp_b = Buf("gp")
    inv = sb("inv", [128, 1], F32); inv_b = Buf("inv")
    bfn = sb("bfn", [8, 1], F32); bfn_b = Buf("bfn")
    ctxneg = sb("ctxneg", [128, 256], F32); ctxneg_b = Buf("ctxneg")
    ctx30 = sb("ctx30", [128, 256], F32); ctx30_b = Buf("ctx30")
    pastm = sb("pastm", [128, 256], F32); pastm_b = Buf("pastm")
    ctxcol = sb("ctxcol", [128, 1], F32); ctxcol_b = Buf("ctxcol")
    wfb = sb("wfb", [128, NCH, 8], BF16); wfb_b = Buf("wfb")
    ones1 = sb("ones1", [128, 1], F32); ones1_b = Buf("ones1")

    AR_W = 22 * 1024 + 320
    arena = sb("arena", [128, AR_W], F32)
    ar_off = [0]

    def carve(nwords):
        a = ar_off[0]
        assert a + nwords <= AR_W, (a, nwords)
        ar_off[0] = a + nwords
        return arena[:, a:a + nwords]

    ps = [es.enter_context(nc.psum_tensor("ps%d" % i, [128, 512], F32)) for i in range(8)]
    ps_b = [Buf("ps%d" % i) for i in range(8)]

    dcnt = [0]

    def dstream(prefix):
        return prefix

    def load_const(dst, src, b, name):
        S.dma("sp", dst, src, writes=[b], stream="c_" + name)

    load_const(ident32[:], c_ident, ident32_b, "ident")
    load_const(rotT[:], c_rotT, rotT_b, "rot")
    load_const(gn[:], g_norm, gn_b, "gn")
    load_const(gp[:], g_ple, gp_b, "gp")
    load_const(inv[:], c_inv, inv_b, "inv")
    load_const(ctxneg[:], c_ctxneg, ctxneg_b, "ctxneg")
    load_const(ctx30[:], c_ctx30, ctx30_b, "ctx30")
    load_const(pastm[:], c_past, pastm_b, "past")
    load_const(ctxcol[:], c_ctxcol, ctxcol_b, "ctxcol")
    load_const(bfn[:], b_f, bfn_b, "bf")
    S.op("pool", lambda e: e.tensor_scalar(out=bfn[:], in0=bfn[:], scalar1=-1.0, scalar2=0.0, op0=ALU.mult, op1=ALU.add),
         reads=[bfn_b], writes=[bfn_b])
    S.op("pool", lambda e: e.memset(ones32[:], 1.0), writes=[ones32_b])
    S.op("pool", lambda e: e.memset(ones1[:], 1.0), writes=[ones1_b])
    S.op("pool", lambda e: e.tensor_copy(out=identb[:], in_=ident32[:]), reads=[ident32_b], writes=[identb_b])
    S.dma("sp", stage[0][:], c_cmask.rearrange("p a b -> p (a b)"), writes=[stage_b[0]], stream="st0")
    S.op("pool", lambda e: e.tensor_copy(out=cmaskb[:].rearrange("p a b -> p (a b)"), in_=stage[0][:]),
         reads=[stage_b[0]], writes=[cmaskb_b])
    S.dma("sp", stage[1][0:16, :], c_E.rearrange("p a b -> p (a b)"), writes=[stage_b[1]], stream="st1")
    S.op("pool", lambda e: e.memset(Eb[:].rearrange("p a b -> p (a b)"), 0.0), writes=[Eb_b])
    S.op("pool", lambda e: e.tensor_copy(out=Eb[0:16].rearrange("p a b -> p (a b)"), in_=stage[1][0:16, :]),
         reads=[stage_b[1]], writes=[Eb_b])
    S.dma("sp", stage[2][0:8, 0:1024], c_oh.rearrange("p a b -> p (a b)"), writes=[stage_b[2]], stream="st2")
    S.op("pool", lambda e: e.memset(ohb[:].rearrange("p a b -> p (a b)"), 0.0), writes=[ohb_b])
    S.op("pool", lambda e: e.tensor_copy(out=ohb[0:8].rearrange("p a b -> p (a b)"), in_=stage[2][0:8, 0:1024]),
         reads=[stage_b[2]], writes=[ohb_b])
    S.dma("sp", stage[0][:, 0:128], w_f.rearrange("p a b -> p (a b)"), writes=[stage_b[0]], stream="st0")
    S.op("pool", lambda e: e.tensor_tensor(out=wfb[:], in0=stage[0][:, 0:128].rearrange("p (a b) -> p a b", b=8),
                                           in1=gn[:].unsqueeze(2).broadcast_to([128, NCH, 8]), op=ALU.mult),
         reads=[stage_b[0], gn_b], writes=[wfb_b])

    def bfv(ap):
        return ap.bitcast(BF16)

    KT = bfv(carve(2048)); KTctx_b = Buf("KTctx"); KTown_b = [Buf("KTown%d" % i) for i in range(4)]
    Vf = bfv(carve(2064))[:, 0:32 * 129].rearrange("p (t d) -> p t d", d=129)
    Vctx_b = Buf("Vctx"); Vown_b = [Buf("Vown%d" % i) for i in range(4)]
    Vones_b = Buf("Vones")
    QT = [bfv(carve(1024)) for _ in range(2)]
    QT_b = [[Buf("QT%d_%d" % (i, j)) for j in range(4)] for i in range(2)]
    zsT = [bfv(carve(1024)) for _ in range(2)]
    zsT_b = [[Buf("zs%d_%d" % (i, j)) for j in range(4)] for i in range(2)]
    cosT = carve(2048); cosT_b = Buf("cosT")
    sinT = carve(2048); sinT_b = Buf("sinT")
    k32 = [carve(512) for _ in range(2)]; k32_b = [Buf("k32_%d" % i) for i in range(2)]
    rt1 = [carve(512) for _ in range(2)]; rt1_b = [Buf("rt1_%d" % i) for i in range(2)]
    PT = [bfv(carve(256)) for _ in range(3)]; PT_b = [Buf("PT%d" % i) for i in range(3)]
    chat_full = bfv(carve(2048)); chat = chat_full[0:8, :]; chat_b = [Buf("chat%d" % i) for i in range(8)]
    negc = carve(256).rearrange("p (t h) -> p t h", h=8); negc_b = [Buf("negc%d" % i) for i in range(8)]
    selbT_full = bfv(carve(1024)); selbT = selbT_full[0:16, :]; selbT_b = [Buf("selbT%d" % i) for i in range(4)]
    gsb = carve(256); gsb_b = Buf("gsb")
    selb = carve(256); selb_b = Buf("selb")
    m8 = carve(128); m8_b = Buf("m8")
    kbar32 = carve(16); kbar32_b = Buf("kbar32")
    kbarb = bfv(carve(8)); kbarb_b = Buf("kbarb")
    On = [bfv(carve(64)) for _ in range(2)]; On_b = [Buf("On%d" % i) for i in range(2)]
    rden = [carve(1) for _ in range(2)]; rden_b = [Buf("rden%d" % i) for i in range(2)]
    GTt = [bfv(carve(256)) for _ in range(2)]; GTt_b = [Buf("GTt%d" % i) for i in range(2)]
    junk = bfv(cosT[:, 0:1024]); junk_b = cosT_b
    ssq = [carve(1) for _ in range(2)]; ssq_b = [Buf("ssq%d" % i) for i in range(2)]
    fe = [carve(512)[0:8, :] for _ in range(2)]; fe_b = [Buf("fe%d" % i) for i in range(2)]
    cst = [carve(512)[0:8, :] for _ in range(2)]; cst_b = [Buf("cst%d" % i) for i in range(2)]
    cslast = carve(1)[0:8, :]; cslast_b = Buf("cslast")
    ones8 = carve(512)[0:8, :]; ones8_b = Buf("ones8")
    abend = ar_off[0]

    S.op("pool", lambda e: e.memset(Vf[:, :, 128:129], 1.0), writes=[Vones_b])
    S.op("pool", lambda e: e.memset(chat_full, 0.0), writes=chat_b)
    S.op("pool", lambda e: e.memset(selbT_full, 0.0), writes=selbT_b)
    S.op("pool", lambda e: e.memset(ones8, 1.0), writes=[ones8_b])
    S.op("pool", lambda e: e.memset(cslast, 0.0), writes=[cslast_b])

    wctr = [0]

    wptr = [0]
    issued = {}

    def _issue(key, slot):
        name, blk, nchunk, gk, pz = key
        gfold, gfold_b = WG[gk]
        wb = wbf[slot % 3]; wbb = wbf_b[slot % 3]
        n = nchunk * 128
        if pz is not None and pz >= 1:
            S.dma("sp", wb[:, 0:nchunk, :].rearrange("p a b -> p (a b)"), WS[name][blk], reads=[WSb[name][blk]], writes=[wbb],
                  stream="wl%d" % (slot % 3))
            return
        src_blk = WSRC[name][blk]
        i = wctr[0]
        wctr[0] += 1
        st = stage[i % 3]; stb = stage_b[i % 3]
        S.dma("sp", st[:, 0:n], src_blk.rearrange("p a b -> p (a b)"), writes=[stb], stream="st%d" % (i % 3))
        ceng = "pool" if slot % 2 == 0 else "dve"
        if gfold is not None:
            S.op(ceng, lambda e: e.tensor_tensor(out=wb[:, 0:nchunk, :], in0=st[:, 0:n].rearrange("p (a b) -> p a b", b=128),
                                                 in1=gfold[:, 0:nchunk].unsqueeze(2).broadcast_to([128, nchunk, 128]), op=ALU.mult),
                 reads=[stb, gfold_b], writes=[wbb])
        else:
            S.op(ceng, lambda e: e.tensor_copy(out=wb[:, 0:nchunk, :], in_=st[:, 0:n].rearrange("p (a b) -> p a b", b=128)),
                 reads=[stb], writes=[wbb])
        if pz == 0:
            S.dma("pool", WS[name][blk], wb[:, 0:nchunk, :].rearrange("p a b -> p (a b)"), reads=[wbb], writes=[WSb[name][blk]],
                  stream="wst%d" % (slot % 3))

    WSRC = {"main": w_main, "ba": w_ba, "bb": w_bb, "o": w_o, "pg": w_pg, "up": w_up}
    WG = {"n": (gn, gn_b), "p": (gp, gp_b), "-": (None, None)}

    def load_weight(name, blk, nchunk, gk, pz=None):
        k = wptr[0]
        wptr[0] += 1
        key = (name, blk, nchunk, gk, pz)
        if rec is not None:
            rec.append(key)
        if k not in issued:
            issued[k] = True
            _issue(key, k)
        for kk in (k + 1, k + 2):
            if wseq is not None and kk < len(wseq) and kk not in issued:
                issued[kk] = True
                _issue(wseq[kk], kk)
        return wbf[k % 3], wbf_b[k % 3]

    psrot = [0]

    def next_ps(lo=0, hi=2):
        k = lo + psrot[0] % (hi - lo)
        psrot[0] += 1
        return k

    xctr = [0]

    def phase0(row0):
        for tl in range(16):
            i = wctr[0]; wctr[0] += 1
            st = stage[i % 3]; stb = stage_b[i % 3]
            S.dma("sp", st[:], x_all[row0 + tl * 128: row0 + (tl + 1) * 128, :], writes=[stb], stream="st%d" % (i % 3))
            sq = ssq[tl % 2]; sqb = ssq_b[tl % 2]
            S.op("act", lambda e, st=st, sq=sq: e.activation(out=junk, in_=st[:], func=AF.Square, accum_out=sq),
                 reads=[stb], writes=[junk_b, sqb])
            S.op("act", lambda e, sq=sq: e.activation(out=sq, in_=sq, func=AF.Sqrt, scale=1.0 / D, bias=1e-6),
                 reads=[sqb], writes=[sqb])
            S.op("dve", lambda e, sq=sq: e.reciprocal(out=sq, in_=sq), reads=[sqb], writes=[sqb])
            S.op("dve", lambda e, st=st, sq=sq: e.tensor_scalar(out=st[:], in0=st[:], scalar1=sq, scalar2=None, op0=ALU.mult),
                 reads=[stb, sqb], writes=[stb])
            base = 4 * (tl % 2)
            for b4 in range(4):
                pk = base + b4
                for c4 in range(4):
                    c = b4 * 4 + c4
                    S.op("pe", lambda e, pk=pk, c4=c4, c=c, st=st: e.transpose(out=ps[pk][:, c4 * 128:(c4 + 1) * 128],
                                                                              in_=st[:, c * 128:(c + 1) * 128], identity=ident32[:]),
                         reads=[stb, ident32_b], writes=[ps_b[pk]])
                eng = "act" if b4 % 2 == 0 else "dve"
                if eng == "act":
                    S.op("act", lambda e, pk=pk, b4=b4, tl=tl: e.activation(
                        out=hT[:, b4 * 4:(b4 + 1) * 4, tl * 128:(tl + 1) * 128],
                        in_=ps[pk][:].rearrange("p (a b) -> p a b", b=128), func=AF.Copy),
                        reads=[ps_b[pk]], writes=[hT_b[tl]])
                else:
                    S.op("dve", lambda e, pk=pk, b4=b4, tl=tl: e.tensor_copy(
                        out=hT[:, b4 * 4:(b4 + 1) * 4, tl * 128:(tl + 1) * 128],
                        in_=ps[pk][:].rearrange("p (a b) -> p a b", b=128)),
                        reads=[ps_b[pk]], writes=[hT_b[tl]])

    def rope_tables(p0):
        i0 = wctr[0]; wctr[0] += 3
        sA, sAb = stage[i0 % 3], stage_b[i0 % 3]
        sB, sBb = stage[(i0 + 1) % 3], stage_b[(i0 + 1) % 3]
        sC, sCb = stage[(i0 + 2) % 3], stage_b[(i0 + 2) % 3]
        S.dma("sp", sA[:].bitcast(I32), pos_all[0:1, p0:p0 + 2048].broadcast_to([128, 2048]), writes=[sAb], stream="st%d" % (i0 % 3))
        S.op("dve", lambda e: e.tensor_copy(out=sB[:], in_=sA[:].bitcast(I32)), reads=[sAb], writes=[sBb])
        S.op("dve", lambda e: e.tensor_scalar(out=sB[:], in0=sB[:], scalar1=inv[:, 0:1], scalar2=None, op0=ALU.mult),
             reads=[sBb, inv_b], writes=[sBb])
        S.op("dve", lambda e: e.tensor_scalar(out=sA[:], in0=sB[:], scalar1=1.0 / TWO_PI, scalar2=MAGIC, op0=ALU.mult, op1=ALU.add),
             reads=[sBb], writes=[sAb])
        S.op("dve", lambda e: e.tensor_scalar(out=sA[:], in0=sA[:], scalar1=-MAGIC, scalar2=None, op0=ALU.add),
             reads=[sAb], writes=[sAb])
        S.op("dve", lambda e: e.scalar_tensor_tensor(out=sB[:], in0=sA[:], scalar=-C1, in1=sB[:], op0=ALU.mult, op1=ALU.add),
             reads=[sAb, sBb], writes=[sBb])
        S.op("dve", lambda e: e.scalar_tensor_tensor(out=sB[:], in0=sA[:], scalar=-C2, in1=sB[:], op0=ALU.mult, op1=ALU.add),
             reads=[sAb, sBb], writes=[sBb])
        S.op("dve", lambda e: e.tensor_scalar(out=sB[:], in0=sB[:], scalar1=PI_SAFE, scalar2=-PI_SAFE, op0=ALU.min, op1=ALU.max),
             reads=[sBb], writes=[sBb])
        S.op("act", lambda e: e.activation(out=sinT, in_=sB[:], func=AF.Sin), reads=[sBb], writes=[sinT_b])
        S.op("dve", lambda e: e.tensor_scalar(out=sC[:], in0=sB[:], scalar1=math.pi / 2, scalar2=-TWO_PI, op0=ALU.is_gt, op1=ALU.mult),
             reads=[sBb], writes=[sCb])
        S.op("dve", lambda e: e.scalar_tensor_tensor(out=sC[:], in0=sB[:], scalar=math.pi / 2, in1=sC[:], op0=ALU.add, op1=ALU.add),
             reads=[sBb, sCb], writes=[sCb])
        S.op("dve", lambda e: e.tensor_scalar(out=sC[:], in0=sC[:], scalar1=PI_SAFE, scalar2=-PI_SAFE, op0=ALU.min, op1=ALU.max),
             reads=[sCb], writes=[sCb])
        S.op("act", lambda e: e.activation(out=cosT, in_=sC[:], func=AF.Sin), reads=[sCb], writes=[cosT_b])

    def proj_fm(wb, wbb, nchunk, rhs_fn, rhs_bufs_fn, epilogue, ntile=4):
        for tt in range(ntile):
            pk = next_ps(0, 2)
            for c in range(nchunk):
                S.op("pe", lambda e, pk=pk, c=c, tt=tt: e.matmul(ps[pk][:], lhsT=wb[:, c, :], rhs=rhs_fn(c, tt),
                                                                start=(c == 0), stop=(c == nchunk - 1)),
                     reads=[wbb] + rhs_bufs_fn(tt), writes=[ps_b[pk]])
            epilogue(tt, pk)

    def hT_rhs(c, tt):
        return hT[:, c, tt * 512:(tt + 1) * 512]

    def hT_bufs(tt):
        return hT_b[tt * 4:(tt + 1) * 4]

    ropectr = [0]

    def rope_epilogue(dst_fn, dst_buf_fn, scale):
        def ep(tt, pk):
            i = ropectr[0] % 2; ropectr[0] += 1
            S.op("act", lambda e: e.activation(out=k32[i], in_=ps[pk][:], func=AF.Copy, scale=scale),
                 reads=[ps_b[pk]], writes=[k32_b[i]])
            pr = next_ps(0, 2)
            S.op("pe", lambda e: e.matmul(ps[pr][:], lhsT=rotT[:], rhs=k32[i], start=True, stop=True),
                 reads=[rotT_b, k32_b[i]], writes=[ps_b[pr]])
            S.op("pool", lambda e: e.tensor_tensor(out=rt1[i], in0=k32[i], in1=cosT[:, tt * 512:(tt + 1) * 512], op=ALU.mult),
                 reads=[k32_b[i], cosT_b], writes=[rt1_b[i]])
            S.op("dve", lambda e: e.tensor_tensor(out=k32[i], in0=ps[pr][:], in1=sinT[:, tt * 512:(tt + 1) * 512], op=ALU.mult),
                 reads=[ps_b[pr], sinT_b], writes=[k32_b[i]])
            S.op("dve", lambda e: e.tensor_tensor(out=dst_fn(tt), in0=rt1[i], in1=k32[i], op=ALU.add),
                 reads=[rt1_b[i], k32_b[i]], writes=[dst_buf_fn(tt)])
        return ep

    def copy_epilogue(dst_fn, dst_buf_fn, scale=1.0, func=None):
        def ep(tt, pk):
            S.op("act", lambda e: e.activation(out=dst_fn(tt), in_=ps[pk][:], func=(func or AF.Copy), scale=scale),
                 reads=[ps_b[pk]], writes=[dst_buf_fn(tt)])
        return ep

    def proj_v(wb, wbb, tile0, vbuf_fn):
        for g4 in range(4):
            pk = next_ps(0, 2)
            for t4 in range(4):
                tk = g4 * 4 + t4
                for c in range(NCH):
                    S.op("pe", lambda e, pk=pk, t4=t4, tk=tk, c=c: e.matmul(
                        ps[pk][:, t4 * 128:(t4 + 1) * 128], lhsT=hT[:, c, tk * 128:(tk + 1) * 128], rhs=wb[:, c, :],
                        start=(c == 0 and t4 == 0), stop=(c == NCH - 1), skip_group_check=True),
                        reads=[wbb, hT_b[tk]], writes=[ps_b[pk]])
            S.op("act", lambda e, pk=pk, g4=g4: e.activation(
                out=Vf[:, tile0 + g4 * 4: tile0 + (g4 + 1) * 4, 0:128],
                in_=ps[pk][:].rearrange("p (a b) -> p a b", b=128), func=AF.Copy),
                reads=[ps_b[pk]], writes=[vbuf_fn(g4)])

    def proj_f(gt0, is_ctx):
        for tt in range(4):
            g = gt0 + tt
            pk = next_ps(0, 2)
            for c in range(NCH):
                S.op("pe", lambda e, pk=pk, c=c, tt=tt: e.matmul(ps[pk][0:8, :], lhsT=wfb[:, c, :], rhs=hT_rhs(c, tt),
                                                                start=(c == 0), stop=(c == NCH - 1)),
                     reads=[wfb_b] + hT_bufs(tt), writes=[ps_b[pk]])
            i = g % 2
            S.op("act", lambda e, pk=pk, i=i: e.activation(out=fe[i], in_=ps[pk][0:8, :], func=AF.Exp, scale=-1.0, bias=bfn[:, 0:1]),
                 reads=[ps_b[pk], bfn_b], writes=[fe_b[i]])
            S.op("act", lambda e, i=i: e.activation(out=fe[i], in_=fe[i], func=AF.Ln, bias=1.0, scale=1.0),
                 reads=[fe_b[i]], writes=[fe_b[i]])
            S.op("dve", lambda e, i=i: e.tensor_tensor_scan(out=cst[i], data0=ones8, data1=fe[i], initial=cslast,
                                                            op0=ALU.mult, op1=ALU.add),
                 reads=[fe_b[i], ones8_b, cslast_b], writes=[cst_b[i]])
            S.op("dve", lambda e, i=i: e.tensor_copy(out=cslast, in_=cst[i][:, 511:512]), reads=[cst_b[i]], writes=[cslast_b])
            S.op("pool", lambda e, i=i, g=g: e.tensor_scalar(out=chat[:, g * 512:(g + 1) * 512], in0=cst[i], scalar1=-1.0, scalar2=0.0,
                                                             op0=ALU.mult, op1=ALU.add),
                 reads=[cst_b[i]], writes=[chat_b[g]])
            pr = next_ps(0, 2)
            for k4 in range(4):
                S.op("pe", lambda e, pr=pr, k4=k4, i=i: e.transpose(out=ps[pr][:, k4 * 8:(k4 + 1) * 8], in_=cst[i][:, k4 * 128:(k4 + 1) * 128],
                                                                    identity=ident32[0:8, 0:8]),
                     reads=[cst_b[i], ident32_b], writes=[ps_b[pr]])
            if is_ctx:
                S.op("dve", lambda e, pr=pr, g=g: e.tensor_scalar(out=negc[:, g * 4:(g + 1) * 4, :],
                                                                  in0=ps[pr][:, 0:32].rearrange("p (a b) -> p a b", b=8),
                                                                  scalar1=ctxcol[:, 0:1], scalar2=None, op0=ALU.add),
                     reads=[ps_b[pr], ctxcol_b], writes=[negc_b[g]])
            else:
                S.op("dve", lambda e, pr=pr, g=g: e.tensor_copy(out=negc[:, g * 4:(g + 1) * 4, :],
                                                                in_=ps[pr][:, 0:32].rearrange("p (a b) -> p a b", b=8)),
                     reads=[ps_b[pr]], writes=[negc_b[g]])

    try:
        phase0(0)
        ck(1, [("hT", hT[:].rearrange("p c t -> p (c t)"), hT_b)])
        rope_tables(0)
        ck(2, [("cos", cosT, [cosT_b]), ("sin", sinT, [sinT_b])])
        proj_f(0, True)
        ck(3, [("chat", chat, chat_b), ("negc", negc.rearrange("p t h -> p (t h)"), negc_b)])
        acnt = 0
        for br in range(2):
            for h in range(nheads):
                par = acnt % 2; acnt += 1
                kblk = (8 + h) if br == 0 else (40 + h)
                vblk = (16 + h) if br == 0 else (48 + h)
                wb, wbb = load_weight("main", kblk, NCH, "n")
                dst = lambda tt, par=par: KT[:, par * CT + tt * 512: par * CT + (tt + 1) * 512]
                dstb = (lambda tt: KTctx_b) if par == 0 else (lambda tt: KTown_b[tt])
                if br == 0:
                    proj_fm(wb, wbb, NCH, hT_rhs, hT_bufs, rope_epilogue(dst, dstb, 1.0))
                else:
                    proj_fm(wb, wbb, NCH, hT_rhs, hT_bufs, copy_epilogue(dst, dstb))
                S.dma("sp", KT_s[br, h], KT[:, par * CT:(par + 1) * CT], reads=([KTctx_b] if par == 0 else KTown_b),
                      writes=[KTs_b[br][h]], stream="kts%d" % par, final=(stop is not None))
                wb, wbb = load_weight("main", vblk, NCH, "n")
                proj_v(wb, wbb, par * 16, (lambda g4: Vctx_b) if par == 0 else (lambda g4: Vown_b[g4]))
                S.dma("sp", V_s[br, h], Vf[:, par * 16:(par + 1) * 16, :], reads=([Vctx_b] if par == 0 else Vown_b) + [Vones_b],
                      writes=[Vs_b[br][h]], stream="vs%d" % par, final=(stop is not None))
                if br == 0 and h == 0:
                    ck(4, [("cos", cosT, [cosT_b]), ("sin", sinT, [sinT_b]), ("chat", chat, chat_b),
                           ("negc", negc.rearrange("p t h -> p (t h)"), negc_b)])
        ck(5)

        phase0(CT)
        rope_tables(CT)
        proj_f(4, False)
        ck(6, [("chat", chat, chat_b), ("negc", negc.rearrange("p t h -> p (t h)"), negc_b)])

        octr = [0]
        sctr = [0]
        ptctr = [0]
        onctr = [0]

        def attention(br, h, qi):
            Q = QT[qi]; Qb = QT_b[qi]; Z = zsT[qi]; Zb = zsT_b[qi]
            for qt in range(4):
                oset = octr[0] % 2; octr[0] += 1
                pX, pY = 4 + 2 * oset, 5 + 2 * oset
                nkt = 16 + 4 * (qt + 1)
                def qk(kt):
                    diag = kt >= 16 + 4 * qt
                    o = kt - (16 + 4 * qt) if diag else 0
                    c0 = o * 128
                    pS = 1 + sctr[0] % 3; sctr[0] += 1
                    kbuf = KTctx_b if kt < 16 else KTown_b[(kt - 16) // 4]
                    S.op("pe", lambda e, pS=pS, kt=kt, qt=qt, c0=c0: e.matmul(
                        ps[pS][:, c0:512], lhsT=KT[:, kt * 128:(kt + 1) * 128], rhs=Q[:, qt * 512 + c0:(qt + 1) * 512],
                        start=True, stop=False), reads=[kbuf, Qb[qt]], writes=[ps_b[pS]])
                    if br == 0:
                        S.op("pe", lambda e, pS=pS, kt=kt, qt=qt, c0=c0, diag=diag: e.matmul(
                            ps[pS][:, c0:512], lhsT=Eb[:, kt // 2, :], rhs=selbT_full[:, qt * 512 + c0:(qt + 1) * 512],
                            start=False, stop=(not diag)), reads=[Eb_b, selbT_b[qt]], writes=[ps_b[pS]])
                    else:
                        S.op("pe", lambda e, pS=pS, kt=kt, qt=qt, c0=c0, diag=diag: e.matmul(
                            ps[pS][:, c0:512], lhsT=ohb[:, h, :], rhs=chat_full[:, CT + qt * 512 + c0: CT + (qt + 1) * 512],
                            start=False, stop=(not diag)), reads=[ohb_b, chat_b[4 + qt]], writes=[ps_b[pS]])
                    if diag:
                        S.op("pe", lambda e, pS=pS, o=o, c0=c0: e.matmul(
                            ps[pS][:, c0:512], lhsT=identb[:], rhs=cmaskb[:, o, c0:512], start=False, stop=True),
                            reads=[identb_b, cmaskb_b], writes=[ps_b[pS]])
                    return (pS, o, c0)

                def expv(kt, info):
                    pS, o, c0 = info
                    vbuf = Vctx_b if kt < 16 else Vown_b[(kt - 16) // 4]
                    pi = ptctr[0] % 3; ptctr[0] += 1
                    if br == 0:
                        S.op("act", lambda e, pS=pS, pi=pi, c0=c0: e.activation(out=PT[pi][:, c0:512], in_=ps[pS][:, c0:512], func=AF.Exp),
                             reads=[ps_b[pS]], writes=[PT_b[pi]])
                    else:
                        S.op("act", lambda e, pS=pS, pi=pi, c0=c0, kt=kt: e.activation(
                            out=PT[pi][:, c0:512], in_=ps[pS][:, c0:512], func=AF.Exp, bias=negc[:, kt, h:h + 1], scale=1.0),
                            reads=[ps_b[pS], negc_b[kt // 4]], writes=[PT_b[pi]])
                    for j in range(o, 4):
                        pO = pX if j < 2 else pY
                        col = (j % 2) * 256
                        last = (kt == 16 + 4 * qt + j)
                        S.op("pe", lambda e, pO=pO, col=col, pi=pi, j=j, kt=kt, last=last: e.matmul(
                            ps[pO][:, col:col + 129], lhsT=PT[pi][:, j * 128:(j + 1) * 128], rhs=Vf[:, kt, :],
                            start=(kt == 0 and j % 2 == 0), stop=last, skip_group_check=True),
                            reads=[PT_b[pi], vbuf, Vones_b], writes=[ps_b[pO]])

                infos = {0: qk(0), 1: qk(1)}
                for kt in range(nkt):
                    if kt + 2 < nkt:
                        infos[kt + 2] = qk(kt + 2)
                    expv(kt, infos.pop(kt))
                gi = onctr[0] % 2; onctr[0] += 1
                pT_ = next_ps(0, 2)
                for j in range(4):
                    pO = pX if j < 2 else pY
                    col = (j % 2) * 256
                    oi = j % 2
                    S.op("dve", lambda e, pO=pO, col=col, oi=oi: e.reciprocal(out=rden[oi], in_=ps[pO][:, col + 128:col + 129]),
                         reads=[ps_b[pO]], writes=[rden_b[oi]])
                    S.op("dve", lambda e, pO=pO, col=col, oi=oi: e.tensor_scalar(out=On[oi], in0=ps[pO][:, col:col + 128], scalar1=rden[oi],
                                                                                 scalar2=None, op0=ALU.mult),
                         reads=[ps_b[pO], rden_b[oi]], writes=[On_b[oi]])
                    S.op("pe", lambda e, pT_=pT_, j=j, oi=oi: e.transpose(out=ps[pT_][:].bitcast(BF16)[:, j * 128:(j + 1) * 128], in_=On[oi],
                                                                          identity=identb[:]),
                         reads=[On_b[oi], identb_b], writes=[ps_b[pT_]])
                S.op("dve", lambda e, pT_=pT_, gi=gi, qt=qt: e.tensor_tensor(out=GTt[gi], in0=ps[pT_][:].bitcast(BF16)[:, 0:512],
                                                                             in1=Z[:, qt * 512:(qt + 1) * 512], op=ALU.mult),
                     reads=[ps_b[pT_], Zb[qt]], writes=[GTt_b[gi]])
                S.dma("sp", GT_s[:, br * 8 + h, qt * 512:(qt + 1) * 512], GTt[gi], reads=[GTt_b[gi]], writes=[GTs_b[br * 8 + h][qt]], stream="gts%d" % gi, final=(stop is not None))

        def moba_select(qi):
            Q = QT[qi]; Qb = QT_b[qi]
            S.op("dve", lambda e: e.tensor_reduce(out=kbar32, in_=KT.rearrange("p (b k) -> p b k", k=256), axis=AX.X, op=ALU.add),
                 reads=[KTctx_b] + KTown_b, writes=[kbar32_b])
            S.op("dve", lambda e: e.tensor_scalar(out=kbarb, in0=kbar32, scalar1=1.0 / 256, scalar2=None, op0=ALU.mult),
                 reads=[kbar32_b], writes=[kbarb_b])
            pg = next_ps(0, 2)
            for j in range(16):
                S.op("pe", lambda e, j=j: e.matmul(ps[pg][:, j * 16:(j + 1) * 16], lhsT=Q[:, j * 128:(j + 1) * 128], rhs=kbarb,
                                                   start=(j == 0), stop=True, skip_group_check=True),
                     reads=[Qb[j // 4], kbarb_b], writes=[ps_b[pg]])
            S.op("dve", lambda e: e.tensor_tensor(out=gsb, in0=ps[pg][:, 0:256], in1=ctxneg[:], op=ALU.add),
                 reads=[ps_b[pg], ctxneg_b], writes=[gsb_b])
            for j in range(16):
                npast = 8 + j // 2
                S.op("dve", lambda e, j=j, npast=npast: e.max(out=m8[:, j * 8:(j + 1) * 8], in_=gsb[:, j * 16:j * 16 + npast]),
                     reads=[gsb_b], writes=[m8_b])
            for j in range(16):
                S.op("dve", lambda e, j=j: e.tensor_scalar(out=selb[:, j * 16:(j + 1) * 16], in0=gsb[:, j * 16:(j + 1) * 16],
                                                           scalar1=m8[:, j * 8 + 2:j * 8 + 3], scalar2=NEG, op0=ALU.is_lt, op1=ALU.mult),
                     reads=[gsb_b, m8_b], writes=[selb_b])
            S.op("dve", lambda e: e.tensor_tensor(out=selb, in0=selb, in1=ctx30[:], op=ALU.add), reads=[selb_b, ctx30_b], writes=[selb_b])
            S.op("dve", lambda e: e.tensor_tensor(out=selb, in0=selb, in1=pastm[:], op=ALU.mult), reads=[selb_b, pastm_b], writes=[selb_b])
            for qt in range(4):
                pk = next_ps(0, 2)
                for j4 in range(4):
                    j = qt * 4 + j4
                    S.op("pe", lambda e, pk=pk, j=j, j4=j4: e.transpose(out=ps[pk][0:16, j4 * 128:(j4 + 1) * 128], in_=selb[:, j * 16:(j + 1) * 16],
                                                                        identity=ident32[:]),
                         reads=[selb_b, ident32_b], writes=[ps_b[pk]])
                S.op("act", lambda e, pk=pk, qt=qt: e.activation(out=selbT[:, qt * 512:(qt + 1) * 512], in_=ps[pk][0:16, :], func=AF.Copy),
                     reads=[ps_b[pk]], writes=[selbT_b[qt]])

        hctr = 0
        for br in range(2):
            for h in range(nheads):
                qi = hctr % 2; hctr += 1
                base = 0 if br == 0 else 32
                qblk, kblk, vblk, zblk = base + h, base + 8 + h, base + 16 + h, base + 24 + h
                S.dma("sp", KT[:, 0:CT], KT_s[br, h], reads=[KTs_b[br][h]], writes=[KTctx_b], stream="ktl")
                S.dma("sp", Vf[:, 0:16, :], V_s[br, h], reads=[Vs_b[br][h]], writes=[Vctx_b], stream="vl")
                wb, wbb = load_weight("main", kblk, NCH, "n")
                dst = lambda tt: KT[:, CT + tt * 512: CT + (tt + 1) * 512]
                dstb = lambda tt: KTown_b[tt]
                if br == 0:
                    proj_fm(wb, wbb, NCH, hT_rhs, hT_bufs, rope_epilogue(dst, dstb, 1.0))
                else:
                    proj_fm(wb, wbb, NCH, hT_rhs, hT_bufs, copy_epilogue(dst, dstb))
                wb, wbb = load_weight("main", qblk, NCH, "n")
                dstq = lambda tt, qi=qi: QT[qi][:, tt * 512:(tt + 1) * 512]
                dstqb = lambda tt, qi=qi: QT_b[qi][tt]
                if br == 0:
                    proj_fm(wb, wbb, NCH, hT_rhs, hT_bufs, rope_epilogue(dstq, dstqb, 1.0 / math.sqrt(HD)))
                else:
                    proj_fm(wb, wbb, NCH, hT_rhs, hT_bufs, copy_epilogue(dstq, dstqb, scale=1.0 / math.sqrt(HD)))
                wb, wbb = load_weight("main", zblk, NCH, "n")
                dstz = lambda tt, qi=qi: zsT[qi][:, tt * 512:(tt + 1) * 512]
                dstzb = lambda tt, qi=qi: zsT_b[qi][tt]
                proj_fm(wb, wbb, NCH, hT_rhs, hT_bufs, copy_epilogue(dstz, dstzb, func=AF.Silu))
                wb, wbb = load_weight("main", vblk, NCH, "n")
                proj_v(wb, wbb, 16, lambda g4: Vown_b[g4])
                if br == 0:
                    moba_select(qi)
                    if h == 0:
                        ck(7, [("selbT", selbT, selbT_b), ("QT", QT[qi], QT_b[qi]), ("KT", KT, [KTctx_b] + KTown_b),
                               ("zsT", zsT[qi], zsT_b[qi]), ("V", Vf.rearrange("p t d -> p (t d)"), [Vctx_b, Vones_b] + Vown_b)])
                attention(br, h, qi)
                if br == 0 and h == 0:
                    ck(8)
                if br == 1 and h == 0:
                    ck(9)

        for blk in range(32):
            wb, wbb = load_weight("main", 64 + blk, NCH, "n")
            gi2 = blk % 2
            proj_fm(wb, wbb, NCH, hT_rhs, hT_bufs,
                    copy_epilogue(lambda tt, gi2=gi2: QT[gi2][:, tt * 512:(tt + 1) * 512], lambda tt, gi2=gi2: QT_b[gi2][tt], func=AF.Sigmoid))
            S.dma("act", SG_s[blk], QT[gi2], reads=QT_b[gi2], writes=[SGs_b[blk]], stream="sgs%d" % gi2)

        S.barrier()
        ar_off[0] = 0
        GTl = bfv(carve(4096)).rearrange("p (c t) -> p c t", t=512); GTl_b = [Buf("GTl%d" % i) for i in range(16)]
        mixT = bfv(carve(4096)).rearrange("p (c t) -> p c t", t=512); mixT_b = [Buf("mixT%d" % i) for i in range(16)]
        xT = carve(8192).rearrange("p (c t) -> p c t", t=512); xT_b = [Buf("xT%d" % i) for i in range(16)]
        pT = bfv(carve(512)).rearrange("p (c t) -> p c t", t=512); pT_b = Buf("pT")
        gfin = carve(2048); gfin_b = Buf("gfin")
        sg = [bfv(carve(256)) for _ in range(4)]; sg_b = [Buf("sg%d" % i) for i in range(4)]
        tm = [carve(512) for _ in range(4)]; tm_b = [Buf("tm%d" % i) for i in range(4)]
        rbc = carve(512); rbc_b = Buf("rbc")
        ssq3 = carve(4); ssq3_b = Buf("ssq3")
        rs3 = carve(1); rs3_b = Buf("rs3")
        ptmp = carve(256); ptmp_b = Buf("ptmp")
        assert ar_off[0] <= AR_W

        S.dma("sp", gfin, g_fin, writes=[gfin_b], stream="c_gfin")

        tmc = [0]
        for tt in range(4):
            t0 = tt * 512
            S.dma("sp", GTl, GT_s[:, :, t0:t0 + 512], reads=[GTs_b[hb][tt] for hb in range(16)], writes=GTl_b, stream="gtl")
            for s4 in range(4):
                i = wctr[0]; wctr[0] += 1
                st = stage[i % 3]; stb = stage_b[i % 3]
                r0 = CT + t0 + s4 * 128
                S.dma("sp", st[:], x_all[r0:r0 + 128, :], writes=[stb], stream="st%d" % (i % 3))
                for b4 in range(4):
                    pk = next_ps(0, 4)
                    for c4 in range(4):
                        c = b4 * 4 + c4
                        S.op("pe", lambda e, pk=pk, c4=c4, c=c, st=st: e.transpose(out=ps[pk][:, c4 * 128:(c4 + 1) * 128],
                                                                                  in_=st[:, c * 128:(c + 1) * 128], identity=ident32[:]),
                             reads=[stb, ident32_b], writes=[ps_b[pk]])
                    S.op("act" if b4 % 2 == 0 else "dve",
                         (lambda e, pk=pk, b4=b4, s4=s4: e.activation(out=xT[:, b4 * 4:(b4 + 1) * 4, s4 * 128:(s4 + 1) * 128],
                                                                      in_=ps[pk][:].rearrange("p (a b) -> p a b", b=128), func=AF.Copy))
                         if b4 % 2 == 0 else
                         (lambda e, pk=pk, b4=b4, s4=s4: e.tensor_copy(out=xT[:, b4 * 4:(b4 + 1) * 4, s4 * 128:(s4 + 1) * 128],
                                                                       in_=ps[pk][:].rearrange("p (a b) -> p a b", b=128))),
                         reads=[ps_b[pk]], writes=xT_b[b4 * 4:(b4 + 1) * 4])
            for s4 in range(4):
                S.dma("sp", ptmp, p_own[t0 + s4 * 128: t0 + (s4 + 1) * 128, :], writes=[ptmp_b], stream="ptmp")
                pk = next_ps(0, 4)
                for c2 in range(2):
                    S.op("pe", lambda e, pk=pk, c2=c2: e.transpose(out=ps[pk][:, c2 * 128:(c2 + 1) * 128],
                                                                   in_=ptmp[:, c2 * 128:(c2 + 1) * 128], identity=ident32[:]),
                         reads=[ptmp_b, ident32_b], writes=[ps_b[pk]])
                S.op("dve", lambda e, pk=pk, s4=s4: e.tensor_copy(out=pT[:, :, s4 * 128:(s4 + 1) * 128],
                                                                  in_=ps[pk][:, 0:256].rearrange("p (a b) -> p a b", b=128)),
                     reads=[ps_b[pk]], writes=[pT_b])

            if tt == ctile:
                ck(10, [("xT", xT.rearrange("p c t -> p (c t)"), xT_b), ("pT", pT.rearrange("p c t -> p (c t)"), [pT_b])])
            for n in range(16):
                res = []
                for which in range(2):
                    si = (2 * n + which) % 4
                    S.dma("sp", sg[si], SG_s[which * 16 + n][:, t0:t0 + 512], reads=[SGs_b[which * 16 + n]], writes=[sg_b[si]],
                          stream="sgl%d" % si)
                res = [None, None]
                for which in (2, 3):
                    if which == 2:
                        wb, wbb = load_weight("ba", n, 8, "-", tt)
                        nch, rf, rb = 8, (lambda c: GTl[:, c, :]), GTl_b[0:8]
                    else:
                        wb, wbb = load_weight("bb", n, 8, "-", tt)
                        nch, rf, rb = 8, (lambda c: GTl[:, 8 + c, :]), GTl_b[8:16]
                    pk = next_ps(0, 8)
                    for c in range(nch):
                        S.op("pe", lambda e, pk=pk, c=c, wb=wb, rf=rf, nch=nch: e.matmul(ps[pk][:], lhsT=wb[:, c, :], rhs=rf(c),
                                                                                        start=(c == 0), stop=(c == nch - 1)),
                             reads=[wbb] + list(rb), writes=[ps_b[pk]])
                    res.append(pk)
                sA, sB = (2 * n) % 4, (2 * n + 1) % 4
                ta = tmc[0] % 4; tb = (tmc[0] + 1) % 4; tmc[0] += 2
                S.op("dve", lambda e, ta=ta, sA=sA, pk=res[2]: e.tensor_tensor(out=tm[ta], in0=ps[pk][:], in1=sg[sA], op=ALU.mult),
                     reads=[ps_b[res[2]], sg_b[sA]], writes=[tm_b[ta]])
                S.op("dve", lambda e, tb=tb, sB=sB, pk=res[3]: e.tensor_tensor(out=tm[tb], in0=ps[pk][:], in1=sg[sB], op=ALU.mult),
                     reads=[ps_b[res[3]], sg_b[sB]], writes=[tm_b[tb]])
                S.op("pool", lambda e, ta=ta, tb=tb, n=n: e.tensor_tensor(out=mixT[:, n, :], in0=tm[ta], in1=tm[tb], op=ALU.add),
                     reads=[tm_b[ta], tm_b[tb]], writes=[mixT_b[n]])

            if tt == ctile:
                ck(11, [("mixT", mixT.rearrange("p c t -> p (c t)"), mixT_b)])
            for m in range(16):
                wb, wbb = load_weight("o", m, NCH, "-", tt)
                pk = next_ps(0, 8)
                for c in range(NCH):
                    S.op("pe", lambda e, pk=pk, c=c, wb=wb: e.matmul(ps[pk][:], lhsT=wb[:, c, :], rhs=mixT[:, c, :],
                                                                    start=(c == 0), stop=(c == NCH - 1)),
                         reads=[wbb, mixT_b[c]], writes=[ps_b[pk]])
                S.op("dve", lambda e, pk=pk, m=m: e.tensor_tensor(out=xT[:, m, :], in0=ps[pk][:], in1=xT[:, m, :], op=ALU.add),
                     reads=[ps_b[pk], xT_b[m]], writes=[xT_b[m]])

            if tt == ctile:
                ck(12, [("x2T", xT.rearrange("p c t -> p (c t)"), xT_b)])
            pq = 5
            for c in range(NCH):
                ti = tmc[0] % 4; tmc[0] += 1
                S.op("act", lambda e, ti=ti, c=c: e.activation(out=tm[ti], in_=xT[:, c, :], func=AF.Square),
                     reads=[xT_b[c]], writes=[tm_b[ti]])
                S.op("pe", lambda e, ti=ti, c=c: e.matmul(ps[pq][:], lhsT=ones32[:], rhs=tm[ti], start=(c == 0), stop=(c == NCH - 1)),
                     reads=[ones32_b, tm_b[ti]], writes=[ps_b[pq]])
            S.op("act", lambda e: e.activation(out=rbc, in_=ps[pq][:], func=AF.Sqrt, scale=1.0 / D, bias=1e-6),
                 reads=[ps_b[pq]], writes=[rbc_b])
            S.op("dve", lambda e: e.reciprocal(out=rbc, in_=rbc), reads=[rbc_b], writes=[rbc_b])
            for c in range(NCH):
                S.op("pool" if c % 2 == 0 else "dve",
                     lambda e, c=c: e.tensor_tensor(out=GTl[:, c, :], in0=xT[:, c, :], in1=rbc, op=ALU.mult),
                     reads=[xT_b[c], rbc_b], writes=[GTl_b[c]])

            if tt == ctile:
                ck(13, [("h2T", GTl.rearrange("p c t -> p (c t)"), GTl_b)])
            for m in range(16):
                wb, wbb = load_weight("pg", m, NCH, "p", tt)
                pk = next_ps(0, 8)
                for c in range(NCH):
                    S.op("pe", lambda e, pk=pk, c=c, wb=wb: e.matmul(ps[pk][:], lhsT=wb[:, c, :], rhs=GTl[:, c, :],
                                                                    start=(c == 0), stop=(c == NCH - 1)),
                         reads=[wbb, GTl_b[c]], writes=[ps_b[pk]])
                si = m % 4
                S.op("act", lambda e, pk=pk, si=si: e.activation(out=sg[si], in_=ps[pk][:], func=AF.Sigmoid),
                     reads=[ps_b[pk]], writes=[sg_b[si]])
                wb2, wbb2 = load_weight("up", m, 2, "-", tt)
                pk2 = next_ps(0, 8)
                for c in range(2):
                    S.op("pe", lambda e, pk2=pk2, c=c, wb2=wb2: e.matmul(ps[pk2][:], lhsT=wb2[:, c, :], rhs=pT[:, c, :],
                                                                        start=(c == 0), stop=(c == 1)),
                         reads=[wbb2, pT_b], writes=[ps_b[pk2]])
                ti = tmc[0] % 4; tmc[0] += 1
                S.op("dve", lambda e, pk2=pk2, si=si, ti=ti: e.tensor_tensor(out=tm[ti], in0=ps[pk2][:], in1=sg[si], op=ALU.mult),
                     reads=[ps_b[pk2], sg_b[si]], writes=[tm_b[ti]])
                S.op("pool", lambda e, ti=ti, m=m: e.tensor_tensor(out=xT[:, m, :], in0=xT[:, m, :], in1=tm[ti], op=ALU.add),
                     reads=[tm_b[ti], xT_b[m]], writes=[xT_b[m]])

            if tt == ctile:
                ck(14, [("x3T", xT.rearrange("p c t -> p (c t)"), xT_b), ("mixT", mixT.rearrange("p c t -> p (c t)"), mixT_b), ("h2T", GTl.rearrange("p c t -> p (c t)"), GTl_b), ("pT", pT.rearrange("p c t -> p (c t)"), [pT_b])])
            for s4 in range(4):
                pks = [4 * (s4 % 2) + b for b in range(4)]
                for b4 in range(4):
                    for c4 in range(4):
                        c = b4 * 4 + c4
                        S.op("pe", lambda e, pk=pks[b4], c4=c4, c=c, s4=s4: e.transpose(out=ps[pk][:, c4 * 128:(c4 + 1) * 128],
                                                                                       in_=xT[:, c, s4 * 128:(s4 + 1) * 128], identity=ident32[:]),
                             reads=[xT_b[c], ident32_b], writes=[ps_b[pks[b4]]])
                i = wctr[0]; wctr[0] += 1
                st = stage[i % 3]; stb = stage_b[i % 3]
                for b4 in range(4):
                    S.op("act", lambda e, pk=pks[b4], b4=b4, st=st: e.activation(out=st[:, b4 * 512:(b4 + 1) * 512], in_=ps[pk][:], func=AF.Square,
                                                                                accum_out=ssq3[:, b4:b4 + 1]),
                         reads=[ps_b[pks[b4]]], writes=[stb, ssq3_b])
                S.op("dve", lambda e: e.tensor_reduce(out=rs3, in_=ssq3, axis=AX.X, op=ALU.add), reads=[ssq3_b], writes=[rs3_b])
                S.op("act", lambda e: e.activation(out=rs3, in_=rs3, func=AF.Sqrt, scale=1.0 / D, bias=1e-6), reads=[rs3_b], writes=[rs3_b])
                S.op("dve", lambda e: e.reciprocal(out=rs3, in_=rs3), reads=[rs3_b], writes=[rs3_b])
                for b4 in range(4):
                    S.op("dve", lambda e, pk=pks[b4], b4=b4, st=st: e.scalar_tensor_tensor(
                        out=st[:, b4 * 512:(b4 + 1) * 512], in0=ps[pk][:], scalar=rs3, in1=gfin[:, b4 * 512:(b4 + 1) * 512],
                        op0=ALU.mult, op1=ALU.mult), reads=[ps_b[pks[b4]], rs3_b, gfin_b], writes=[stb])
                r0 = t0 + s4 * 128
                S.dma("sp", out_d[r0:r0 + 128, :], st[:], reads=[stb], writes=[], stream="st%d" % (i % 3), final=True)
            if tt == ctile:
                ck(15)


    except _Stop:
        pass
    if rec is None:
        S.emit()
    es.close()
    return nc


def _blk(w, nchunk):
    K, N = w.shape
    return np.ascontiguousarray(w.reshape(nchunk, 128, N // 128, 128).transpose(2, 1, 0, 3))


def _consts(half):
    ident = np.eye(128, dtype=np.float32)
    rotT = np.zeros((128, 128), np.float32)
    for m in range(64):
        rotT[m + 64, m] = -1.0
    for m in range(64, 128):
        rotT[m - 64, m] = 1.0
    j = np.arange(128) % 64
    inv = (10000.0 ** (-(2.0 * j) / 128.0)).astype(np.float32).reshape(128, 1)
    cmask = np.zeros((128, 4, 512), np.float32)
    s = np.arange(128)[:, None]
    t = np.arange(512)[None, :]
    for o in range(4):
        cmask[:, o, :] = np.where(o * 128 + s <= t, 0.0, NEG)
    E = np.zeros((16, 16, 128), np.float32)
    for i in range(16):
        E[i, i, :] = 1.0
    oh = np.zeros((8, 8, 128), np.float32)
    for i in range(8):
        oh[i, i, :] = 1.0
    ctxneg = np.zeros((128, 16, 16), np.float32)
    ctx30 = np.zeros((128, 16, 16), np.float32)
    past = np.zeros((128, 16, 16), np.float32)
    for jj in range(16):
        past[:, jj, 0:8 + jj // 2] = 1.0
    if half == 0:
        ctxneg[:, :, 0:8] = -1e30
        ctx30[:, :, 0:8] = NEG
    ctxcol = np.full((128, 1), NEG if half == 0 else 0.0, np.float32)
    return {"c_ident": ident, "c_rotT": rotT, "c_inv": inv, "c_cmask": cmask, "c_E": E, "c_oh": oh,
            "c_ctxneg": ctxneg.reshape(128, 256), "c_ctx30": ctx30.reshape(128, 256),
            "c_past": past.reshape(128, 256), "c_ctxcol": ctxcol}


_NC_CACHE = {}


def make_in_maps(x, p, positions, g_norm, w_in, b_f, w_branch_a, w_branch_b, w_out, g_ple, w_ple_gate, w_ple_up, g_final):
    x = np.asarray(x, np.float32); p = np.asarray(p, np.float32); positions = np.asarray(positions, np.int32)
    w_in = np.asarray(w_in, np.float32)[0]
    wm = np.concatenate([w_in[:, 0:7168], w_in[:, 7176:12296]], axis=1)
    shared = {
        "w_main": _blk(wm, 16),
        "w_f": np.ascontiguousarray(w_in[:, 7168:7176].reshape(16, 128, 8).transpose(1, 0, 2)),
        "w_ba": _blk(np.asarray(w_branch_a, np.float32)[0], 8),
        "w_bb": _blk(np.asarray(w_branch_b, np.float32)[0], 8),
        "w_o": _blk(np.asarray(w_out, np.float32)[0], 16),
        "w_pg": _blk(np.asarray(w_ple_gate, np.float32)[0], 16),
        "w_up": _blk(np.asarray(w_ple_up, np.float32)[0], 2),
        "g_norm": np.ascontiguousarray(np.asarray(g_norm, np.float32)[0].reshape(16, 128).T),
        "g_ple": np.ascontiguousarray(np.asarray(g_ple, np.float32)[0].reshape(16, 128).T),
        "g_fin": np.ascontiguousarray(np.broadcast_to(np.asarray(g_final, np.float32)[None, :], (128, D))),
        "b_f": np.asarray(b_f, np.float32)[0].reshape(8, 1),
    }
    maps = []
    for core in range(8):
        b, half = core // 2, core % 2
        m = dict(shared)
        m["x_all"] = np.ascontiguousarray(np.concatenate([x[b, 0:CT], x[b, half * T:(half + 1) * T]], axis=0))
        m["pos_all"] = np.ascontiguousarray(np.concatenate([positions[b, 0:CT], positions[b, half * T:(half + 1) * T]])[None, :])
        m["p_own"] = np.ascontiguousarray(p[0, b, half * T:(half + 1) * T])
        m.update(_consts(half))
        maps.append(m)
    return maps


def kernel(x, p, positions, g_norm, w_in, b_f, w_branch_a, w_branch_b, w_out, g_ple, w_ple_gate, w_ple_up, g_final):
    maps = make_in_maps(x, p, positions, g_norm, w_in, b_f, w_branch_a, w_branch_b, w_out, g_ple, w_ple_gate, w_ple_up, g_final)
    if "nc" not in _NC_CACHE:
        rec = []
        build_program(rec=rec)
        _NC_CACHE["nc"] = build_program(wseq=rec)
    nc = _NC_CACHE["nc"]
    res = run_bass_kernel_spmd(nc, maps, core_ids=list(range(8)))
    out = np.zeros((4, 2 * T, D), np.float32)
    for core in range(8):
        b, half = core // 2, core % 2
        out[b, half * T:(half + 1) * T] = res.results[core]["out"]
    return out
```
